# Optimizing a Trainium2 kernel written in Bass

```python
import math
import jax, jax.numpy as jnp
from jax import lax
import numpy as np

D_MODEL = 2048
BATCH = 16
SEQ = 2048
DEPTH = 2

D_MIX = D_MODEL
EPS = 1e-5
POOL_WIDTH = D_MIX // 4
POOL_WINDOWS = (2, 4, 8, 16)
POOL_GROUPS = len(POOL_WINDOWS)
POOL_GROUP_DIM = POOL_WIDTH // POOL_GROUPS
MLA_HEADS = 8
MLA_NOPE_DIM = 128
MLA_ROPE_DIM = 64
MLA_V_DIM = 128
MLA_WIDTH = MLA_HEADS * MLA_V_DIM
MLA_Q_RANK = 512
MLA_KV_RANK = 256
MLA_QK_DIM = MLA_NOPE_DIM + MLA_ROPE_DIM
ROPE_THETA = 10000.0
Q_BLOCK = 128
SGU_WIDTH = D_MIX - POOL_WIDTH - MLA_WIDTH
SGU_HEADS = 4
SGU_HEAD_DIM = SGU_WIDTH // SGU_HEADS
SGU_CHUNK = 128
ALPHA = (2.0 * DEPTH) ** 0.25
BETA = (8.0 * DEPTH) ** -0.25

COL_A_X = POOL_WIDTH
COL_A_G = POOL_WIDTH
COL_B_CQ = MLA_Q_RANK
COL_B_CKV = MLA_KV_RANK
COL_B_KR = MLA_ROPE_DIM
COL_B_G = MLA_WIDTH
COL_C_UV = 2 * SGU_WIDTH
COL_C_G = SGU_WIDTH
D_IN_COLS = COL_A_X + COL_A_G + COL_B_CQ + COL_B_CKV + COL_B_KR + COL_B_G + COL_C_UV + COL_C_G
SPLIT_IDX = (
    COL_A_X,
    COL_A_X + COL_A_G,
    COL_A_X + COL_A_G + COL_B_CQ,
    COL_A_X + COL_A_G + COL_B_CQ + COL_B_CKV,
    COL_A_X + COL_A_G + COL_B_CQ + COL_B_CKV + COL_B_KR,
    COL_A_X + COL_A_G + COL_B_CQ + COL_B_CKV + COL_B_KR + COL_B_G,
    COL_A_X + COL_A_G + COL_B_CQ + COL_B_CKV + COL_B_KR + COL_B_G + COL_C_UV,
)

kernel_name = "hybrid_pool_mla_sgu_deepnorm"


def _layer_norm(x, g, b):
    xf = x.astype(jnp.float32)
    mu = jnp.mean(xf, axis=-1, keepdims=True)
    var = jnp.mean(jnp.square(xf - mu), axis=-1, keepdims=True)
    return ((xf - mu) * lax.rsqrt(var + EPS) * g.astype(jnp.float32) + b.astype(jnp.float32)).astype(x.dtype)


def _rms_norm(x, g):
    xf = x.astype(jnp.float32)
    ms = jnp.mean(jnp.square(xf), axis=-1, keepdims=True)
    return (xf * lax.rsqrt(ms + EPS) * g.astype(jnp.float32)).astype(x.dtype)


def _rope(x, cos, sin):
    half = MLA_ROPE_DIM // 2
    xf = x.astype(jnp.float32)
    x1, x2 = xf[..., :half], xf[..., half:]
    out = jnp.concatenate([x1 * cos - x2 * sin, x2 * cos + x1 * sin], axis=-1)
    return out.astype(x.dtype)


def _pool_mixer(xa, w_pool, pool_scale):
    B, S, _ = xa.shape
    xg = xa.reshape(B, S, POOL_GROUPS, POOL_GROUP_DIM)
    cs = jnp.cumsum(xg.astype(jnp.float32), axis=1)
    t = jnp.arange(1, S + 1, dtype=jnp.float32)[None, :, None]
    pooled = []
    for gi, w in enumerate(POOL_WINDOWS):
        c = cs[:, :, gi]
        prev = jnp.pad(c, ((0, 0), (w, 0), (0, 0)))[:, :S]
        pooled.append((c - prev) / jnp.minimum(t, float(w)))
    pooled = jnp.stack(pooled, axis=2).astype(xa.dtype) - xg
    y = jnp.einsum('bsgc,gcd->bsgd', pooled, w_pool)
    return y.reshape(B, S, POOL_WIDTH) * pool_scale


def _mla(cq, ckv, kr, cos, sin, q_norm_g, w_uq, kv_norm_g, w_ukv):
    B, S, _ = cq.shape
    q = jnp.einsum('bsr,rd->bsd', _rms_norm(cq, q_norm_g), w_uq).reshape(B, S, MLA_HEADS, MLA_QK_DIM)
    q_nope = q[..., :MLA_NOPE_DIM]
    q_rope = _rope(q[..., MLA_NOPE_DIM:], cos, sin)
    kv = jnp.einsum('bsr,rd->bsd', _rms_norm(ckv, kv_norm_g), w_ukv).reshape(B, S, MLA_HEADS, MLA_NOPE_DIM + MLA_V_DIM)
    k_nope = kv[..., :MLA_NOPE_DIM]
    v = kv[..., MLA_NOPE_DIM:]
    k_rope = _rope(kr[:, :, None, :], cos, sin)[:, :, 0]
    scale = MLA_QK_DIM ** -0.5
    n_blocks = S // Q_BLOCK
    qn_b = q_nope.reshape(B, n_blocks, Q_BLOCK, MLA_HEADS, MLA_NOPE_DIM).transpose(1, 0, 2, 3, 4)
    qr_b = q_rope.reshape(B, n_blocks, Q_BLOCK, MLA_HEADS, MLA_ROPE_DIM).transpose(1, 0, 2, 3, 4)
    key_idx = jnp.arange(S)

    def one_block(args):
        i, qn, qr = args
        s = (jnp.einsum('bqhd,bkhd->bhqk', qn, k_nope)
             + jnp.einsum('bqhr,bkr->bhqk', qr, k_rope)).astype(jnp.float32) * scale
        q_idx = i * Q_BLOCK + jnp.arange(Q_BLOCK)
        causal = key_idx[None, :] <= q_idx[:, None]
        s = jnp.where(causal[None, None], s, -jnp.inf)
        p = jax.nn.softmax(s, axis=-1).astype(v.dtype)
        return jnp.einsum('bhqk,bkhd->bqhd', p, v)

    out = lax.map(one_block, (jnp.arange(n_blocks), qn_b, qr_b))
    return out.transpose(1, 0, 2, 3, 4).reshape(B, S, MLA_WIDTH)


def _sgu(uv, sgu_norm_g, sgu_norm_b, w_s, b_s):
    B, S, _ = uv.shape
    uv = jax.nn.gelu(uv, approximate=False)
    u, v = uv[..., :SGU_WIDTH], uv[..., SGU_WIDTH:]
    v = _layer_norm(v, sgu_norm_g, sgu_norm_b)
    n_chunks = S // SGU_CHUNK
    v = v.reshape(B, n_chunks, SGU_CHUNK, SGU_HEADS, SGU_HEAD_DIM)
    tril = jnp.tril(jnp.ones((SGU_CHUNK, SGU_CHUNK), dtype=w_s.dtype))
    mixed = jnp.einsum('hts,bcshd->bcthd', w_s * tril, v) + b_s.T[None, None, :, :, None]
    return u * mixed.reshape(B, S, SGU_WIDTH)


def setup_inputs(seed: int = 0) -> dict:
    key = jax.random.key(seed)
    ks = jax.random.split(key, 20)
    f32 = jnp.float32
    nrm = lambda k, shape, s: jax.random.normal(k, shape, f32) * s
    x = jax.random.normal(ks[0], (BATCH, SEQ, D_MODEL), f32)
    offsets = jax.random.randint(ks[1], (BATCH, 1), 0, 1024, dtype=jnp.int32)
    positions = (offsets + jnp.arange(SEQ, dtype=jnp.int32)[None, :]).astype(jnp.int32)
    return {
        "x": x,
        "positions": positions,
        "ln_in_g": 1.0 + nrm(ks[2], (D_MODEL,), 0.02),
        "ln_in_b": nrm(ks[3], (D_MODEL,), 0.02),
        "w_in": nrm(ks[4], (DEPTH, D_MODEL, D_IN_COLS), D_MODEL ** -0.5),
        "pool_w": nrm(ks[5], (DEPTH, POOL_GROUPS, POOL_GROUP_DIM, POOL_GROUP_DIM), POOL_GROUP_DIM ** -0.5),
        "pool_scale": 1.0 + nrm(ks[6], (DEPTH, POOL_WIDTH), 0.1),
        "q_norm_g": 1.0 + nrm(ks[7], (DEPTH, MLA_Q_RANK), 0.02),
        "w_uq": nrm(ks[8], (DEPTH, MLA_Q_RANK, MLA_HEADS * MLA_QK_DIM), MLA_Q_RANK ** -0.5),
        "kv_norm_g": 1.0 + nrm(ks[9], (DEPTH, MLA_KV_RANK), 0.02),
        "w_ukv": nrm(ks[10], (DEPTH, MLA_KV_RANK, MLA_HEADS * (MLA_NOPE_DIM + MLA_V_DIM)), MLA_KV_RANK ** -0.5),
        "sgu_norm_g": 1.0 + nrm(ks[11], (DEPTH, SGU_WIDTH), 0.02),
        "sgu_norm_b": nrm(ks[12], (DEPTH, SGU_WIDTH), 0.02),
        "sgu_w": nrm(ks[13], (DEPTH, SGU_HEADS, SGU_CHUNK, SGU_CHUNK), SGU_CHUNK ** -0.5),
        "sgu_b": 1.0 + nrm(ks[14], (DEPTH, SGU_HEADS, SGU_CHUNK), 0.1),
        "w_out": nrm(ks[15], (DEPTH, D_MIX, D_MODEL), BETA * D_MIX ** -0.5),
        "b_out": nrm(ks[16], (DEPTH, D_MODEL), 0.02),
        "ln_post_g": 1.0 + nrm(ks[17], (DEPTH, D_MODEL), 0.02),
        "ln_post_b": nrm(ks[18], (DEPTH, D_MODEL), 0.02),
    }


def reference(x, positions, ln_in_g, ln_in_b, w_in, pool_w, pool_scale, q_norm_g, w_uq, kv_norm_g, w_ukv,
              sgu_norm_g, sgu_norm_b, sgu_w, sgu_b, w_out, b_out, ln_post_g, ln_post_b):
    half = MLA_ROPE_DIM // 2
    inv_freq = ROPE_THETA ** (-jnp.arange(half, dtype=jnp.float32) / half)
    ang = positions.astype(jnp.float32)[..., None] * inv_freq
    cos = jnp.cos(ang)[:, :, None, :]
    sin = jnp.sin(ang)[:, :, None, :]

    h = _layer_norm(x, ln_in_g, ln_in_b)
    for l in range(DEPTH):
        proj = jnp.einsum('bsd,de->bse', h, w_in[l])
        a_x, a_g, b_cq, b_ckv, b_kr, b_g, c_uv, c_g = jnp.split(proj, SPLIT_IDX, axis=-1)
        y_a = _pool_mixer(a_x, pool_w[l], pool_scale[l]) * jax.nn.silu(a_g)
        y_b = _mla(b_cq, b_ckv, b_kr, cos, sin, q_norm_g[l], w_uq[l], kv_norm_g[l], w_ukv[l]) * jax.nn.silu(b_g)
        y_c = _sgu(c_uv, sgu_norm_g[l], sgu_norm_b[l], sgu_w[l], sgu_b[l]) * jax.nn.silu(c_g)
        y = jnp.concatenate([y_a, y_b, y_c], axis=-1)
        y = jnp.einsum('bse,ed->bsd', y, w_out[l]) + b_out[l]
        h = _layer_norm(ALPHA * h + y, ln_post_g[l], ln_post_b[l])
    return h
```

```python
import math
import types
from contextlib import ExitStack
import numpy as np
import concourse.bass as bass
import concourse.mybir as mybir
from concourse.bass_utils import run_bass_kernel_spmd

F32 = mybir.dt.float32
BF16 = mybir.dt.bfloat16
I32 = mybir.dt.int32
AF = mybir.ActivationFunctionType
ALU = mybir.AluOpType

D = 2048
NCOL = 4416
T = 256
EPS = 1e-5
NSLOT = 4
DEBUG = False
import os
VAR = "nodefer"
SLOT_ELEMS = 4096


def _freeze(fn):
    if fn is None or fn.__closure__ is None:
        return fn
    cells = []
    for c in fn.__closure__:
        try:
            cells.append(types.CellType(c.cell_contents))
        except ValueError:
            cells.append(c)
    return types.FunctionType(fn.__code__, fn.__globals__, fn.__name__, fn.__defaults__, tuple(cells))


class Sched:
    ENG = ("pe", "act", "dve", "pool", "sp")

    def __init__(self):
        self.ops = []
        self.last_w = {}
        self.readers = {}
        self.dma_last = {}

    ALIAS = {"cqT": ["R4a", "R4b"], "cvf": ["R4a", "R4b"], "sgt0": ["R4a"], "sgt1": ["R4b"]}

    def add(self, eng, fn, reads=(), writes=(), dma=None):
        idx = len(self.ops)
        reads = [k for r in reads for k in self.ALIAS.get(r, [r])]
        writes = [k for r in writes for k in self.ALIAS.get(r, [r])]
        deps = set()
        for r in reads:
            w = self.last_w.get(r)
            if w is not None:
                deps.add(w)
        for wk in writes:
            w = self.last_w.get(wk)
            if w is not None:
                deps.add(w)
            lastr = {}
            for rd in self.readers.get(wk, ()):
                o = self.ops[rd]
                if o["dma"] is not None:
                    deps.add(rd)
                else:
                    lastr[o["eng"]] = rd
            deps.update(lastr.values())
        if dma is not None:
            p = self.dma_last.get(dma)
            if p is not None:
                deps.add(p)
            self.dma_last[dma] = idx
        deps.discard(idx)
        for r in reads:
            self.readers.setdefault(r, []).append(idx)
        for wk in writes:
            self.last_w[wk] = idx
            self.readers[wk] = []
        self.ops.append(dict(eng=eng, fn=_freeze(fn), deps=deps, dma=dma, idx=idx, signal=False))
        return idx

    def emit(self, nc, ctx):
        ops = self.ops
        for op in ops:
            for d in op["deps"]:
                p = ops[d]
                if p["dma"] is None and p["eng"] == "pe" and op["eng"] == "pe" and op["dma"] is None:
                    continue
                p["signal"] = True
        cnt = {}
        for op in ops:
            if op["dma"] is not None:
                k = ("dma", op["dma"])
                cnt[k] = cnt.get(k, 0) + 16
                op["sem"] = k
                op["val"] = cnt[k]
            elif op["signal"]:
                k = ("eng", op["eng"])
                cnt[k] = cnt.get(k, 0) + 1
                op["sem"] = k
                op["val"] = cnt[k]
        sems = {}
        for n, k in enumerate(cnt):
            sems[k] = ctx.enter_context(nc.semaphore("sem%d" % n))
        by_eng = {e: [o for o in ops if o["eng"] == e] for e in self.ENG}
        block = ctx.enter_context(nc.Block())

        def run(engh, lst):
            waited = {}
            for op in lst:
                need = {}
                for d in op["deps"]:
                    p = ops[d]
                    if "sem" not in p:
                        continue
                    need[p["sem"]] = max(need.get(p["sem"], 0), p["val"])
                for k, v in need.items():
                    if waited.get(k, 0) >= v:
                        continue
                    engh.wait_ge(sems[k], v)
                    waited[k] = v
                if op["fn"] is None:
                    continue
                ins = op["fn"](engh)
                if "sem" in op:
                    ins.then_inc(sems[op["sem"]], 16 if op["dma"] is not None else 1)

        @block.tensor
        def _(e):
            run(e, by_eng["pe"])

        @block.scalar
        def _(e):
            run(e, by_eng["act"])

        @block.vector
        def _(e):
            run(e, by_eng["dve"])

        @block.gpsimd
        def _(e):
            run(e, by_eng["pool"])

        @block.sync
        def _(e):
            run(e, by_eng["sp"])


def slab_plan():
    pl = []
    pl += [("in", 1024, 256, 16), ("in", 1280, 256, 16), ("in", 1536, 256, 16), ("in", 1792, 64, 16)]
    pl += [("in", 3392, 256, 16), ("in", 3648, 256, 16)]
    pl += [("in", 0, 256, 16), ("in", 256, 256, 16)]
    pl += [("in", 2880, 256, 16), ("in", 3136, 256, 16)]
    pl += [("uq", 0, 768, 4), ("uq", 768, 768, 4), ("ukv", 0, 2048, 2)]
    pl += [("in", 512, 256, 16), ("in", 768, 256, 16)]
    pl += [("in", 3904, 256, 16), ("in", 4160, 256, 16)]
    pl += [("in", 1856 + 256 * i, 256, 16) for i in range(4)]
    pl += [("out", 256 * i, 256, 16) for i in range(8)]
    return pl


PLAN = slab_plan()
NSLAB = len(PLAN)


def build_program(NSEQ, S, L):
    NT = S // T
    NKT = S // 128
    ALPHA = (2.0 * L) ** 0.25
    SCALE = 192 ** -0.5
    nc = bass.Bass("TRN2", target_bir_lowering=False)
    dt_in = lambda name, shape, dt=F32: nc.dram_tensor(name, shape, dt, kind="ExternalInput").ap()
    x_d = dt_in("x", [NSEQ, S, D])
    pos_d = dt_in("positions", [NSEQ, S], I32)
    lnin_g = dt_in("ln_in_g", [D])
    lnin_b = dt_in("ln_in_b", [D])
    w_in = dt_in("w_in", [L, D, NCOL])
    pool_w = dt_in("pool_w", [L, 4, 128, 128])
    w_uq = dt_in("w_uq", [L, 512, 1536])
    w_ukv = dt_in("w_ukv", [L, 256, 2048])
    sgu_wT = dt_in("sgu_wT", [L, 4, 128, 128])
    sgu_bias = dt_in("sgu_b", [L, 512])
    w_out = dt_in("w_out", [L, D, D])
    b_out = dt_in("b_out", [L, D])
    lnp_g = dt_in("ln_post_g", [L, D])
    lnp_b = dt_in("ln_post_b", [L, D])
    pp_d = dt_in("pparams", [128, L, 20])
    c_ident = dt_in("c_ident", [128, 128])
    c_mask = dt_in("c_mask", [128, 128])
    c_invc = dt_in("c_invc", [64])
    c_rope = dt_in("c_rope", [64, 4])
    out_d = nc.dram_tensor("out", [NSEQ, S, D], F32, kind="ExternalOutput").ap()
    wb_d = nc.dram_tensor("wb", [L, NSLAB, 128, SLOT_ELEMS], BF16, kind="Internal").ap()
    hbuf_d = nc.dram_tensor("hbuf", [NSEQ, S, D], F32, kind="Internal").ap()
    h0buf_d = nc.dram_tensor("h0buf", [NSEQ, S, D], F32, kind="Internal").ap()
    rope_d = nc.dram_tensor("ropetab", [NSEQ, 2, 64, S], F32, kind="Internal").ap()

    dbg_d = nc.dram_tensor("dbg", [NSEQ, NT, 128, 16, T], BF16, kind="ExternalOutput").ap() if DEBUG else None
    S_ = Sched()
    add = S_.add
    with ExitStack() as ctx:
        sb = lambda name, shape, dt: ctx.enter_context(nc.sbuf_tensor(name, shape, dt))
        Kc = sb("Kc", [128, 8, S], BF16)
        Vc = sb("Vc", [128, NKT, 1024], BF16)
        KRc = sb("KRc", [128, S], BF16)
        hs = [sb("hs%d" % i, [128, D], F32) for i in range(2)]
        hT = sb("hT", [128, 16, T], BF16)
        yT = sb("yT", [128, 16, T], BF16)
        R4 = sb("R4", [128, 4 * T], F32)
        cqT = R4[:, :].rearrange("p (c t) -> p c t", c=4)
        cqn = sb("cqn", [128, 4, T], BF16)
        ckvT = sb("ckvT", [128, 2, T], F32)
        ckvn = sb("ckvn", [128, 2, T], BF16)
        rq = sb("rq", [128, T], F32)
        rkv = sb("rkv", [128, T], F32)
        qn = sb("qn", [128, 8, T], BF16)
        qr = sb("qr", [128, 8, T], BF16)
        cs2 = sb("cs2", [64, T], F32)
        sn2 = sb("sn2", [64, T], F32)
        rt = [sb("rt%d" % i, [64, T], F32) for i in range(4)]
        rti = sb("rti", [64, T], I32)
        axT = sb("axT", [128, 4, 16 + T], F32)
        tA = sb("tA", [128, 16 + T], F32)
        tB = sb("tB", [128, 16 + T], F32)
        pl = [sb("pl%d" % i, [128, T], BF16) for i in range(4)]
        cu = sb("cu", [128, 4, T], BF16)
        cvf = R4[:, :].rearrange("p (s n) -> p s n", s=2)
        cvn = sb("cvn", [128, 2, 512], BF16)
        PT = [sb("PT%d" % i, [128, 2, T], BF16) for i in range(2)]
        rc = sb("rc", [128, T], F32)
        ot = sb("ot", [128, T], F32)
        sgt = [R4[:, i * 2 * T:(i + 1) * 2 * T].rearrange("p (c t) -> p c t", c=2) for i in range(2)]
        stmp = sb("stmp", [128, 512], F32)
        stmp2 = sb("stmp2", [128, 512], F32)
        slots = [sb("slot%d" % i, [128, SLOT_ELEMS], BF16) for i in range(NSLOT)]
        lng = sb("lng", [128, D], F32)
        lnb = sb("lnb", [128, D], F32)
        identf = sb("identf", [128, 128], F32)
        ones_bf = sb("ones_bf", [128, 128], BF16)
        maskf = sb("maskf", [128, 128], F32)
        mask_bf = sb("mask_bf", [128, 128], BF16)
        epsb = sb("epsb", [128, 1], F32)
        nhalf = sb("nhalf", [128, 1], F32)
        invc = sb("invc", [128, 64], F32)
        pp = sb("pp", [128, L, 20], F32)
        crope = sb("crope", [64, 4], F32)
        bsb = sb("bsb", [128, 512], F32)
        rsw = sb("rsw", [128, 512], F32)
        wpool = sb("wpool", [128, 4, 128], BF16)
        wsT = sb("wsT", [128, 4, 128], BF16)
        bo = sb("bo", [128, D], BF16)
        sel33 = sb("sel33", [128, 128], BF16)
        stats = sb("stats", [128, 24], F32)
        mv = sb("mv", [128, 2], F32)
        rs1 = sb("rs1", [128, 1], F32)
        banks = [ctx.enter_context(nc.psum_tensor("bank%d" % i, [128, 512], F32)) for i in range(8)]
        PB = [("P0", banks[0]), ("P1", banks[1]), ("X0", banks[6])]
        SB_ = [("S0", banks[2]), ("S1", banks[3])]
        OB = [("O0", banks[4]), ("O1", banks[5])]
        X0 = ("X0", banks[6])
        X1 = ("X1", banks[7])


        def conv_src(l, kind, c0, ncols, kc):
            if kind == "in":
                return w_in[l].rearrange("(k p) n -> p k n", p=128)[:, :, c0:c0 + ncols]
            if kind == "uq":
                return w_uq[l].rearrange("(k p) n -> p k n", p=128)[:, :, c0:c0 + ncols]
            if kind == "ukv":
                return w_ukv[l].rearrange("(k p) n -> p k n", p=128)
            return w_out[l].rearrange("(k p) n -> p k n", p=128)[:, :, c0:c0 + ncols]

        nconv = [0]

        def conv_one(l, si):
            kind, c0, ncols, kc = PLAN[si]
            dst = wb_d[l, si, :, 0:kc * ncols].rearrange("p (k n) -> p k n", k=kc)
            src = conv_src(l, kind, c0, ncols, kc)
            add("pool", lambda e: e.dma_start(out=dst, in_=src), writes=[("wb", l, si)], dma="cv%d" % (nconv[0] % 6))
            nconv[0] += 1

        def conv_layers(ls):
            for l in ls:
                for si in range(NSLAB):
                    conv_one(l, si)

        pending_conv = []

        add("sp", lambda e: e.dma_start(out=identf[:], in_=c_ident[:, :]), writes=["identf"], dma="c0")
        add("sp", lambda e: e.dma_start(out=maskf[:], in_=c_mask[:, :]), writes=["maskf"], dma="c1")
        add("sp", lambda e: e.dma_start(out=invc[:], in_=c_invc.partition_broadcast(128)), writes=["invc"], dma="c2")
        add("sp", lambda e: e.dma_start(out=pp[:], in_=pp_d[:, :, :]), writes=["pp"], dma="c3")
        add("sp", lambda e: e.dma_start(out=crope[:], in_=c_rope[:, :]), writes=["crope"], dma="c4")
        add("dve", lambda e: e.tensor_copy(out=mask_bf[:], in_=maskf[:]), reads=["maskf"], writes=["mask_bf"])
        add("dve", lambda e: e.memset(ones_bf[:], 1.0), writes=["ones_bf"])
        add("dve", lambda e: e.memset(epsb[:], float(EPS)), writes=["epsb"])
        add("dve", lambda e: e.memset(nhalf[:], -0.5), writes=["nhalf"])
        add("dve", lambda e: e.memset(sel33[:], 0.0), writes=["sel33"])
        add("dve", lambda e: e.memset(sel33[0:1, :], 1.0), writes=["sel33"])
        add("dve", lambda e: e.memset(sel33[32:33, :], 1.0), writes=["sel33"])
        add("dve", lambda e: e.memset(bo[:], 0.0), writes=["bo"])
        add("dve", lambda e: e.memset(KRc[64:128, :], 0.0), writes=["KRc"])
        add("dve", lambda e: e.memset(qr[64:128, :, :], 0.0), writes=["qr"])

        TWO_PI = 2.0 * math.pi
        C1 = 6.28125
        C2 = TWO_PI - C1
        def rope_chunk(s, ch):
            if True:
                c0 = ch * T
                add("pool", lambda e, s=s, c0=c0: e.dma_start(out=rti[:], in_=pos_d[s, c0:c0 + T].partition_broadcast(64)),
                    writes=["rti"], dma="rp")
                add("dve", lambda e: e.tensor_copy(out=rt[0][:], in_=rti[:]), reads=["rti"], writes=["rt0"])
                add("dve", lambda e: e.tensor_scalar(out=rt[0][:], in0=rt[0][:], scalar1=crope[:, 0:1], scalar2=None, op0=ALU.mult),
                    reads=["rt0", "crope"], writes=["rt0"])
                for which in range(2):
                    off = math.pi / 2 if which == 0 else 0.0
                    add("dve", lambda e, off=off: e.tensor_scalar(out=rt[1][:], in0=rt[0][:], scalar1=float(off), scalar2=None, op0=ALU.add),
                        reads=["rt0"], writes=["rt1"])
                    add("dve", lambda e: e.tensor_scalar(out=rt[2][:], in0=rt[1][:], scalar1=float(1.0 / TWO_PI), scalar2=None, op0=ALU.mult),
                        reads=["rt1"], writes=["rt2"])
                    add("dve", lambda e: e.tensor_copy(out=rti[:], in_=rt[2][:]), reads=["rt2"], writes=["rti"])
                    add("dve", lambda e: e.tensor_copy(out=rt[2][:], in_=rti[:]), reads=["rti"], writes=["rt2"])
                    add("dve", lambda e: e.scalar_tensor_tensor(out=rt[1][:], in0=rt[2][:], scalar=float(-C1), in1=rt[1][:], op0=ALU.mult, op1=ALU.add),
                        reads=["rt2", "rt1"], writes=["rt1"])
                    add("dve", lambda e: e.scalar_tensor_tensor(out=rt[1][:], in0=rt[2][:], scalar=float(-C2), in1=rt[1][:], op0=ALU.mult, op1=ALU.add),
                        reads=["rt2", "rt1"], writes=["rt1"])
                    add("dve", lambda e: e.tensor_scalar(out=rt[1][:], in0=rt[1][:], scalar1=float(math.pi), scalar2=float(-math.pi), op0=ALU.min, op1=ALU.max),
                        reads=["rt1"], writes=["rt1"])
                    add("act", lambda e: e.activation(out=rt[3][:], in_=rt[1][:], func=AF.Sin), reads=["rt1"], writes=["rt3"])
                    if which == 1:
                        add("dve", lambda e: e.tensor_scalar(out=rt[3][:], in0=rt[3][:], scalar1=crope[:, 1:2], scalar2=None, op0=ALU.mult),
                            reads=["rt3", "crope"], writes=["rt3"])
                    add("pool", lambda e, s=s, which=which, c0=c0: e.dma_start(out=rope_d[s, which, :, c0:c0 + T], in_=rt[3][:]),
                        reads=["rt3"], writes=[("rope", s, ch)], dma="rp")

        stream = []
        order = [(s, l, i) for s in range(NSEQ) for l in range(L) for i in range(NT)]
        for (s, l, i) in order:
            for si in range(NSLAB):
                stream.append((l, si))
        st_state = dict(issued=0, cur=0)

        def issue_loads(upto):
            while st_state["issued"] < min(upto, len(stream)):
                n = st_state["issued"]
                l, si = stream[n]
                slot = n % NSLOT
                kind, c0, ncols, kc = PLAN[si]
                ne = kc * ncols
                add("sp", lambda e, l=l, si=si, slot=slot, ne=ne: e.dma_start(out=slots[slot][:, 0:ne], in_=wb_d[l, si, :, 0:ne]),
                    reads=[("wb", l, si)], writes=["slot%d" % slot], dma="slot%d" % slot)
                st_state["issued"] += 1

        def next_slab():
            if pending_conv:
                conv_one(*pending_conv.pop(0))
            n = st_state["cur"]
            issue_loads(n + NSLOT)
            st_state["cur"] += 1
            slot = n % NSLOT
            l, si = stream[n]
            kind, c0, ncols, kc = PLAN[si]
            view = slots[slot][:, 0:kc * ncols].rearrange("p (k n) -> p k n", k=kc)
            return "slot%d" % slot, view, slots[slot]

        pbi = [0]

        def next_pb():
            r = PB[pbi[0] % 3]
            pbi[0] += 1
            return r

        ln_state = [None]

        def ensure_ln(which, l):
            key = (which, l)
            if ln_state[0] == key:
                return
            ln_state[0] = key
            gsrc = lnin_g if which == "in" else lnp_g[l]
            bsrc = lnin_b if which == "in" else lnp_b[l]
            add("pool", lambda e: e.dma_start(out=lng[:], in_=gsrc.partition_broadcast(128)), writes=["lng"], dma="lng")
            add("pool", lambda e: e.dma_start(out=lnb[:], in_=bsrc.partition_broadcast(128)), writes=["lnb"], dma="lnb")

        def layer_setup(l):
            add("pool", lambda e: e.dma_start(out=wpool[:], in_=pool_w[l].rearrange("g c d -> c g d")), writes=["wpool"], dma="wpool")
            add("pool", lambda e: e.dma_start(out=wsT[:], in_=sgu_wT[l].rearrange("h s t -> s h t")), writes=["wsT"], dma="wsT")
            for h in range(4):
                add("dve", lambda e, h=h: e.tensor_tensor(out=wsT[:, h, :], in0=wsT[:, h, :], in1=mask_bf[:], op=ALU.mult),
                    reads=["wsT", "mask_bf"], writes=["wsT"])
            add("pool", lambda e: e.dma_start(out=bsb[:], in_=sgu_bias[l].partition_broadcast(128)), writes=["bsb"], dma="bsb")
            add("pe", lambda e: e.matmul(X1[1][:, :], lhsT=ones_bf[:], rhs=wsT[:].rearrange("p h t -> p (h t)"), start=True, stop=True),
                reads=["wsT", "ones_bf"], writes=["X1"])
            add("dve", lambda e: e.tensor_copy(out=rsw[:], in_=X1[1][:, :]), reads=["X1"], writes=["rsw"])
            for hd in range(4):
                add("dve", lambda e, hd=hd: e.scalar_tensor_tensor(out=rsw[:, hd * 128:(hd + 1) * 128], in0=rsw[:, hd * 128:(hd + 1) * 128],
                                                                 scalar=pp[:, l, 14 + hd:15 + hd], in1=bsb[:, hd * 128:(hd + 1) * 128],
                                                                 op0=ALU.mult, op1=ALU.add), reads=["rsw", "bsb", "pp"], writes=["rsw"])
            for q4 in range(4):
                add("pool", lambda e, q4=q4: e.dma_start(out=stmp[0:1, :], in_=b_out[l, q4 * 512:(q4 + 1) * 512].rearrange("(a n) -> a n", a=1)),
                    writes=["stmp"], dma="bo_a")
                add("pool", lambda e, q4=q4: e.dma_start(out=stmp[32:33, :], in_=b_out[l, q4 * 512:(q4 + 1) * 512].rearrange("(a n) -> a n", a=1)),
                    writes=["stmp"], dma="bo_b")
                sl = slice(q4 * 512, (q4 + 1) * 512)
                add("dve", lambda e, sl=sl: e.tensor_copy(out=bo[0:1, sl], in_=stmp[0:1, :]), reads=["stmp"], writes=["bo"])
                add("dve", lambda e, sl=sl: e.tensor_copy(out=bo[32:33, sl], in_=stmp[32:33, :]), reads=["stmp"], writes=["bo"])
                add("dve", lambda e, sl=sl: e.tensor_tensor(out=stmp[32:33, :], in0=stmp[32:33, :], in1=bo[32:33, sl], op=ALU.subtract),
                    reads=["stmp", "bo"], writes=["stmp"])
                add("dve", lambda e, sl=sl: e.tensor_copy(out=bo[32:33, sl], in_=stmp[32:33, :]), reads=["stmp"], writes=["bo"])

        def ln_rows(buf, key, nch, width, use_pool=True):
            for c in range(nch):
                add("dve", lambda e, c=c: e.bn_stats(out=stats[:, c * 6:(c + 1) * 6], in_=buf[:, c * width:(c + 1) * width]),
                    reads=[key], writes=["stats"])
            add("dve", lambda e: e.bn_aggr(out=mv[:], in_=stats[:, 0:nch * 6]), reads=["stats"], writes=["mv"])
            if use_pool:
                add("dve", lambda e: e.tensor_scalar(out=rs1[:], in0=mv[:, 1:2], scalar1=float(EPS), scalar2=None, op0=ALU.add), reads=["mv"], writes=["rs1"])
                add("pool", lambda e: e.tensor_tensor(out=rs1[:], in0=rs1[:], in1=nhalf[:, 0:1], op=ALU.pow), reads=["rs1", "nhalf"], writes=["rs1"])
            else:
                add("act", lambda e: e.activation(out=rs1[:], in_=mv[:, 1:2], func=AF.Sqrt, bias=epsb[:, 0:1]), reads=["mv", "epsb"], writes=["rs1"])
                add("dve", lambda e: e.reciprocal(out=rs1[:], in_=rs1[:]), reads=["rs1"], writes=["rs1"])

        def proj_fm(hkey, slabkey, view, nblk, bank, bkey, M=128):
            for blk in range(nblk):
                for kc in range(16):
                    add("pe", lambda e, blk=blk, kc=kc: e.matmul(
                        bank[0:M, blk * T:(blk + 1) * T], lhsT=view[:, kc, blk * M:(blk + 1) * M], rhs=hT[:, kc, :],
                        start=(kc == 0), stop=(kc == 15)),
                        reads=[hkey, slabkey], writes=[bkey])

        cur_layer = [None]

        def stage0a(s, l, i, part, sts=(0, 1)):
            t0 = i * T
            if part == 1 and l == 0:
                ensure_ln("in", 0)
            for st in sts:
                r0 = t0 + st * 128
                src = x_d[s, r0:r0 + 128, :] if l == 0 else hbuf_d[s, r0:r0 + 128, :]
                hk = "hs%d" % st
                rd = [("hbuf", s, r0)] if l > 0 else []
                if part == 0:
                    add("pool", lambda e: e.dma_start(out=hs[st][:], in_=src), reads=rd, writes=[hk], dma="ld%d" % st)
                elif l == 0:
                    ln_rows(hs[st], hk, 4, 512)
                    add("dve", lambda e: e.scalar_tensor_tensor(out=hs[st][:], in0=hs[st][:], scalar=mv[:, 0:1], in1=lng[:],
                                                              op0=ALU.subtract, op1=ALU.mult), reads=[hk, "mv", "lng"], writes=[hk])
                    add("dve", lambda e: e.scalar_tensor_tensor(out=hs[st][:], in0=hs[st][:], scalar=rs1[:, 0:1], in1=lnb[:],
                                                              op0=ALU.mult, op1=ALU.add), reads=[hk, "rs1", "lnb"], writes=[hk])
                    add("pool", lambda e: e.dma_start(out=h0buf_d[s, r0:r0 + 128, :], in_=hs[st][:]),
                        reads=[hk], writes=[("h0", s, r0)], dma="h0s%d" % st)

        def stage0b(s, l, i):
            t0 = i * T
            for st in range(2):
                hk = "hs%d" % st
                for g4 in range(4):
                    bk, bank = X0 if "x0" in VAR else next_pb()
                    for c in range(4):
                        ch = g4 * 4 + c
                        add("pe", lambda e: e.transpose(out=bank[:, c * 128:(c + 1) * 128], in_=hs[st][:, ch * 128:(ch + 1) * 128],
                                                      identity=identf[:]),
                            reads=[hk, "identf"], writes=[bk])
                    add("act", lambda e: e.activation(out=hT[:, g4 * 4:(g4 + 1) * 4, st * 128:(st + 1) * 128],
                                                    in_=bank[:, :].rearrange("p (c t) -> p c t", c=4), func=AF.Copy),
                        reads=[bk], writes=["hT"])
            add("pool", lambda e: e.dma_start(out=cs2[:], in_=rope_d[s, 0, :, t0:t0 + T]), reads=[("rope", s, i)], writes=["cs2"], dma="cs2")
            add("pool", lambda e: e.dma_start(out=sn2[:], in_=rope_d[s, 1, :, t0:t0 + T]), reads=[("rope", s, i)], writes=["sn2"], dma="sn2")

        def tile_gen(s, l, i):
            t0 = i * T
            if cur_layer[0] != l:
                cur_layer[0] = l
                layer_setup(l)
            deferred = []

            for j in range(2):
                sk, view, _ = next_slab()
                bk, bank = next_pb()
                proj_fm("hT", sk, view, 2, bank, bk)
                add("act", lambda e, j=j, bank=bank: e.activation(out=cqT[:, 2 * j:2 * j + 2, :], in_=bank[:, :].rearrange("p (c t) -> p c t", c=2), func=AF.Copy),
                    reads=[bk], writes=["cqT"])
                add("dve", lambda e, j=j, bank=bank: e.tensor_tensor(out=cqn[:, 2 * j:2 * j + 2, :], in0=bank[:, :].rearrange("p (c t) -> p c t", c=2),
                                                                    in1=cqT[:, 2 * j:2 * j + 2, :], op=ALU.mult),
                    reads=[bk, "cqT"], writes=["cqn"])
            for c in range(4):
                add("pe", lambda e, c=c: e.matmul(X1[1][:, 0:T], lhsT=ones_bf[:], rhs=cqn[:, c, :], start=(c == 0), stop=(c == 3)),
                    reads=["cqn", "ones_bf"], writes=["X1"])
            add("act", lambda e: e.activation(out=rq[:], in_=X1[1][:, 0:T], func=AF.Sqrt, scale=1.0 / 512, bias=epsb[:, 0:1]), reads=["X1", "epsb"], writes=["rq"])
            add("dve", lambda e: e.reciprocal(out=rq[:], in_=rq[:]), reads=["rq"], writes=["rq"])
            for c in range(4):
                add("dve", lambda e, c=c: e.scalar_tensor_tensor(out=cqn[:, c, :], in0=cqT[:, c, :], scalar=pp[:, l, c:c + 1], in1=rq[:],
                                                              op0=ALU.mult, op1=ALU.mult), reads=["cqT", "rq", "pp"], writes=["cqn"])
            sk, view, _ = next_slab()
            bk, bank = next_pb()
            proj_fm("hT", sk, view, 2, bank, bk)
            add("act", lambda e, bank=bank: e.activation(out=ckvT[:, :, :], in_=bank[:, :].rearrange("p (c t) -> p c t", c=2), func=AF.Copy),
                reads=[bk], writes=["ckvT"])
            add("dve", lambda e, bank=bank: e.tensor_tensor(out=ckvn[:, :, :], in0=bank[:, :].rearrange("p (c t) -> p c t", c=2), in1=ckvT[:, :, :], op=ALU.mult),
                reads=[bk, "ckvT"], writes=["ckvn"])
            for c in range(2):
                add("pe", lambda e, c=c: e.matmul(X1[1][:, T:2 * T], lhsT=ones_bf[:], rhs=ckvn[:, c, :], start=(c == 0), stop=(c == 1)),
                    reads=["ckvn", "ones_bf"], writes=["X1"])
            add("act", lambda e: e.activation(out=rkv[:], in_=X1[1][:, T:2 * T], func=AF.Sqrt, scale=1.0 / 256, bias=epsb[:, 0:1]), reads=["X1", "epsb"], writes=["rkv"])
            add("dve", lambda e: e.reciprocal(out=rkv[:], in_=rkv[:]), reads=["rkv"], writes=["rkv"])
            for c in range(2):
                add("dve", lambda e, c=c: e.scalar_tensor_tensor(out=ckvn[:, c, :], in0=ckvT[:, c, :], scalar=pp[:, l, 4 + c:5 + c], in1=rkv[:],
                                                              op0=ALU.mult, op1=ALU.mult), reads=["ckvT", "rkv", "pp"], writes=["ckvn"])

            rope_ctr = [0]

            def rope_apply(bank, bkey, col0, dst, dkey):
                b0 = 2 * (rope_ctr[0] % 2)
                rope_ctr[0] += 1
                ra, rb = rt[b0], rt[b0 + 1]
                ka, kb = "rt%d" % b0, "rt%d" % (b0 + 1)
                add("act", lambda e: e.activation(out=ra[:], in_=bank[0:64, col0:col0 + T], func=AF.Copy), reads=[bkey], writes=[ka])
                add("act", lambda e: e.activation(out=rb[0:32, :], in_=bank[32:64, col0:col0 + T], func=AF.Copy), reads=[bkey], writes=[kb])
                add("act", lambda e: e.activation(out=rb[32:64, :], in_=bank[0:32, col0:col0 + T], func=AF.Copy), reads=[bkey], writes=[kb])
                add("dve", lambda e: e.tensor_tensor(out=ra[:], in0=ra[:], in1=cs2[:], op=ALU.mult), reads=[ka, "cs2"], writes=[ka])
                add("dve", lambda e: e.tensor_tensor(out=rb[:], in0=rb[:], in1=sn2[:], op=ALU.mult), reads=[kb, "sn2"], writes=[kb])
                add("dve", lambda e: e.tensor_tensor(out=dst, in0=ra[:], in1=rb[:], op=ALU.add), reads=[ka, kb], writes=[dkey])

            sk, view, _ = next_slab()
            bk, bank = next_pb()
            for kc in range(16):
                add("pe", lambda e, kc=kc, bank=bank, view=view: e.matmul(bank[0:64, 0:T], lhsT=view[:, kc, 0:64], rhs=hT[:, kc, :], start=(kc == 0), stop=(kc == 15)),
                    reads=["hT", sk], writes=[bk])
            rope_apply(bank, bk, 0, KRc[0:64, t0:t0 + T], "KRc")

            yield "a1"
            for j in range(2):
                sk, view, _ = next_slab()
                bk, bank = next_pb()
                for st in range(2):
                    for kc in range(16):
                        add("pe", lambda e, st=st, kc=kc, bank=bank, view=view: e.matmul(
                            bank[:, st * 256:(st + 1) * 256], lhsT=hT[:, kc, st * 128:(st + 1) * 128], rhs=view[:, kc, :],
                            start=(kc == 0), stop=(kc == 15)), reads=["hT", sk], writes=[bk])
                add("act", lambda e, j=j, bank=bank: e.activation(out=cvf[:, :, j * 256:(j + 1) * 256], in_=bank[:, :].rearrange("p (s n) -> p s n", s=2), func=AF.Gelu),
                    reads=[bk], writes=["cvf"])
            for st in range(2):
                add("dve", lambda e, st=st: e.bn_stats(out=stats[:, 0:6], in_=cvf[:, st, :]), reads=["cvf"], writes=["stats"])
                add("dve", lambda e: e.bn_aggr(out=mv[:], in_=stats[:, 0:6]), reads=["stats"], writes=["mv"])
                add("act", lambda e: e.activation(out=rs1[:], in_=mv[:, 1:2], func=AF.Sqrt, bias=epsb[:, 0:1]), reads=["mv", "epsb"], writes=["rs1"])
                add("dve", lambda e: e.reciprocal(out=rs1[:], in_=rs1[:]), reads=["rs1"], writes=["rs1"])
                add("dve", lambda e, st=st: e.tensor_scalar(out=cvn[:, st, :], in0=cvf[:, st, :], scalar1=mv[:, 0:1], scalar2=rs1[:, 0:1],
                                                        op0=ALU.subtract, op1=ALU.mult), reads=["cvf", "mv", "rs1"], writes=["cvn"])
            if i == 0:
                add("dve", lambda e: e.memset(axT[:, :, 0:16], 0.0), writes=["axT"])
            else:
                add("dve", lambda e: e.tensor_copy(out=axT[:, :, 0:16], in_=axT[:, :, T:T + 16]), reads=["axT"], writes=["axT"])
            for j in range(2):
                sk, view, _ = next_slab()
                bk, bank = next_pb()
                proj_fm("hT", sk, view, 2, bank, bk)
                add("act", lambda e, j=j, bank=bank: e.activation(out=axT[:, 2 * j:2 * j + 2, 16:16 + T], in_=bank[:, :].rearrange("p (c t) -> p c t", c=2), func=AF.Copy),
                    reads=[bk], writes=["axT"])
            W = 16 + T
            for g in range(4):
                w = 2 << g
                add("dve", lambda e, g=g: e.tensor_tensor(out=tA[:, 1:W], in0=axT[:, g, 1:W], in1=axT[:, g, 0:W - 1], op=ALU.add),
                    reads=["axT"], writes=["tA"])
                cur, ck, oth, ok_ = tA, "tA", tB, "tB"
                sh = 2
                lo = 1
                while sh < w:
                    lo2 = lo + sh
                    add("dve", lambda e, cur=cur, oth=oth, sh=sh, lo2=lo2: e.tensor_tensor(out=oth[:, lo2:W], in0=cur[:, lo2:W], in1=cur[:, lo2 - sh:W - sh], op=ALU.add),
                        reads=[ck], writes=[ok_])
                    cur, ck, oth, ok_ = oth, ok_, cur, ck
                    lo = lo2
                    sh *= 2
                plb = pl[g]
                pk = "pl%d" % g
                add("dve", lambda e, cur=cur, g=g, w=w, plb=plb: e.scalar_tensor_tensor(out=plb[:], in0=cur[:, 16:W], scalar=1.0 / w, in1=axT[:, g, 16:W],
                                                                                    op0=ALU.mult, op1=ALU.subtract), reads=[ck, "axT"], writes=[pk])
                if i == 0:
                    add("dve", lambda e, cur=cur, g=g: e.tensor_tensor(out=stmp[:, 0:16], in0=cur[:, 16:32], in1=invc[:, g * 16:(g + 1) * 16], op=ALU.mult),
                        reads=[ck, "invc"], writes=["stmp"])
                    add("dve", lambda e, g=g, plb=plb: e.tensor_tensor(out=plb[:, 0:16], in0=stmp[:, 0:16], in1=axT[:, g, 16:32], op=ALU.subtract),
                        reads=["stmp", "axT", pk], writes=[pk])
                def pool_mm(g=g, plb=plb, pk=pk):
                    xk, xbank = next_pb()
                    add("pe", lambda e: e.matmul(xbank[:, 0:T], lhsT=wpool[:, g, :], rhs=plb[:], start=True, stop=True),
                        reads=[pk, "wpool"], writes=[xk])
                    add("act", lambda e: e.activation(out=yT[:, g, :], in_=xbank[:, 0:T], func=AF.Identity, scale=pp[:, l, 6 + g:7 + g]),
                        reads=[xk, "pp"], writes=[("yT", g)])
                deferred.append(pool_mm)

            for j in range(2):
                sk, view, _ = next_slab()
                bk, bank = next_pb()
                proj_fm("hT", sk, view, 2, bank, bk)
                add("act", lambda e, j=j, bank=bank: e.activation(out=cu[:, 2 * j:2 * j + 2, :], in_=bank[:, :].rearrange("p (c t) -> p c t", c=2), func=AF.Gelu),
                    reads=[bk], writes=["cu"])
            for fn_ in deferred:
                fn_()
            deferred = []
            def sgu_mm():
                for st in range(2):
                    for hd in range(4):
                        add("pe", lambda e, st=st, hd=hd: e.matmul(X1[1][:, hd * 128:(hd + 1) * 128], lhsT=cvn[:, st, hd * 128:(hd + 1) * 128], rhs=wsT[:, hd, :],
                                                               start=True, stop=True), reads=["cvn", "wsT"], writes=["X1"])
                    for hd in range(4):
                        add("dve", lambda e, hd=hd: e.scalar_tensor_tensor(out=stmp2[:, hd * 128:(hd + 1) * 128], in0=X1[1][:, hd * 128:(hd + 1) * 128],
                                                                         scalar=pp[:, l, 10 + hd:11 + hd], in1=rsw[:, hd * 128:(hd + 1) * 128],
                                                                         op0=ALU.mult, op1=ALU.add), reads=["X1", "rsw", "pp"], writes=["stmp2"])
                    add("dve", lambda e, st=st: e.tensor_tensor(out=yT[:, 12:16, st * 128:(st + 1) * 128], in0=stmp2[:, :].rearrange("p (h t) -> p h t", h=4),
                                                               in1=cu[:, :, st * 128:(st + 1) * 128], op=ALU.mult),
                        reads=["stmp2", "cu"], writes=[("yT", 12), ("yT", 13), ("yT", 14), ("yT", 15)])
            sgu_mm()

            yield "ac"
            for half in range(2):
                sk, view, _ = next_slab()
                for hp in range(2):
                    bk, bank = next_pb()
                    for hh2 in range(2):
                        hh = hp * 2 + hh2
                        for kc in range(4):
                            add("pe", lambda e, kc=kc, hh=hh, hh2=hh2, bank=bank, view=view: e.matmul(
                                bank[:, hh2 * T:(hh2 + 1) * T], lhsT=view[:, kc, hh * 192:hh * 192 + 128], rhs=cqn[:, kc, :],
                                start=(kc == 0), stop=(kc == 3)), reads=["cqn", sk], writes=[bk])
                    h0 = half * 4 + hp * 2
                    add("act", lambda e, h0=h0, bank=bank: e.activation(out=qn[:, h0:h0 + 2, :], in_=bank[:, :].rearrange("p (c t) -> p c t", c=2), func=AF.Copy),
                        reads=[bk], writes=["qn"])
                for hp in range(2):
                    bk, bank = next_pb()
                    for hh2 in range(2):
                        hh = hp * 2 + hh2
                        for kc in range(4):
                            add("pe", lambda e, kc=kc, hh=hh, hh2=hh2, bank=bank, view=view: e.matmul(
                                bank[0:64, hh2 * T:(hh2 + 1) * T], lhsT=view[:, kc, hh * 192 + 128:hh * 192 + 192], rhs=cqn[:, kc, :],
                                start=(kc == 0), stop=(kc == 3)), reads=["cqn", sk], writes=[bk])
                    for hh2 in range(2):
                        h = half * 4 + hp * 2 + hh2
                        rope_apply(bank, bk, hh2 * T, qr[0:64, h, :], "qr")
            sk, view, slot_t = next_slab()
            v5 = slot_t[:, :].rearrange("p (k h two d) -> p k h two d", k=2, h=8, two=2)
            for hp in range(4):
                bk, bank = next_pb()
                for hh2 in range(2):
                    h = hp * 2 + hh2
                    for kc in range(2):
                        add("pe", lambda e, kc=kc, h=h, hh2=hh2, bank=bank: e.matmul(
                            bank[:, hh2 * T:(hh2 + 1) * T], lhsT=v5[:, kc, h, 0, :], rhs=ckvn[:, kc, :], start=(kc == 0), stop=(kc == 1)),
                            reads=["ckvn", sk], writes=[bk])
                add("act", lambda e, hp=hp, bank=bank: e.activation(out=Kc[:, 2 * hp:2 * hp + 2, t0:t0 + T], in_=bank[:, :].rearrange("p (c t) -> p c t", c=2), func=AF.Copy),
                    reads=[bk], writes=["Kc"])
            for st in range(2):
                for hg in range(2):
                    bk, bank = next_pb()
                    for kc in range(2):
                        add("pe", lambda e, kc=kc, st=st, hg=hg, bank=bank: e.matmul(
                            bank[:, :].rearrange("p (h d) -> p h d", h=4), lhsT=ckvn[:, kc, st * 128:(st + 1) * 128], rhs=v5[:, kc, hg * 4:(hg + 1) * 4, 1, :],
                            start=(kc == 0), stop=(kc == 1)), reads=["ckvn", sk], writes=[bk])
                    kt = t0 // 128 + st
                    add("act", lambda e, kt=kt, hg=hg, bank=bank: e.activation(out=Vc[:, kt, hg * 512:(hg + 1) * 512], in_=bank[:, :], func=AF.Copy),
                        reads=[bk], writes=["Vc"])

            yield "front"
            steps = [(h, p) for h in range(8) for p in range(i + 1)]

            def emit_S(k):
                h, p = steps[k]
                bk, bank = SB_[k % 2]
                for jj in range(2):
                    j = 2 * p + jj
                    cs = slice(128, T) if (p == i and jj == 1) else slice(0, T)
                    add("pe", lambda e, h=h, j=j, jj=jj, cs=cs, bank=bank: e.matmul(
                        bank[:, jj * T + cs.start:jj * T + cs.stop], lhsT=Kc[:, h, j * 128:(j + 1) * 128], rhs=qn[:, h, cs], start=True, stop=False),
                        reads=["Kc", "qn"], writes=[bk])
                    add("pe", lambda e, h=h, j=j, jj=jj, cs=cs, bank=bank: e.matmul(
                        bank[:, jj * T + cs.start:jj * T + cs.stop], lhsT=KRc[:, j * 128:(j + 1) * 128], rhs=qr[:, h, cs], start=False, stop=True),
                        reads=["KRc", "qr"], writes=[bk])

            def emit_exp(k):
                h, p = steps[k]
                bk, bank = SB_[k % 2]
                ptk = "PT%d" % (k % 2)
                pt = PT[k % 2]
                if p < i:
                    add("act", lambda e: e.activation(out=pt[:, :, :], in_=bank[:, :].rearrange("p (j t) -> p j t", j=2), func=AF.Exp, scale=float(SCALE)),
                        reads=[bk], writes=[ptk])
                else:
                    add("act", lambda e: e.activation(out=pt[:, 0, :], in_=bank[:, 0:T], func=AF.Exp, scale=float(SCALE)), reads=[bk], writes=[ptk])
                    add("act", lambda e: e.activation(out=pt[:, 1, 128:T], in_=bank[:, T + 128:2 * T], func=AF.Exp, scale=float(SCALE)), reads=[bk], writes=[ptk])
                    add("dve", lambda e: e.tensor_tensor(out=pt[:, 0, 0:128], in0=pt[:, 0, 0:128], in1=mask_bf[:], op=ALU.mult), reads=[ptk, "mask_bf"], writes=[ptk])
                    add("dve", lambda e: e.tensor_tensor(out=pt[:, 1, 128:T], in0=pt[:, 1, 128:T], in1=mask_bf[:], op=ALU.mult), reads=[ptk, "mask_bf"], writes=[ptk])

            def emit_PV(k):
                h, p = steps[k]
                ok, obank = OB[h % 2]
                ptk = "PT%d" % (k % 2)
                pt = PT[k % 2]
                for jj in range(2):
                    j = 2 * p + jj
                    cs = slice(128, T) if (p == i and jj == 1) else slice(0, T)
                    firstmm = (p == 0 and jj == 0)
                    lastmm = (p == i and jj == 1)
                    add("pe", lambda e, h=h, j=j, jj=jj, cs=cs, firstmm=firstmm: e.matmul(
                        obank[:, cs], lhsT=Vc[:, j, h * 128:(h + 1) * 128], rhs=pt[:, jj, cs], start=firstmm, stop=False, skip_group_check=True),
                        reads=["Vc", ptk], writes=[ok])
                    add("pe", lambda e, jj=jj, cs=cs, lastmm=lastmm: e.matmul(
                        obank[:, T + cs.start:T + cs.stop], lhsT=ones_bf[:], rhs=pt[:, jj, cs], start=False, stop=lastmm, skip_group_check=True),
                        reads=["ones_bf", ptk], writes=[ok])
                if p == i:
                    add("dve", lambda e: e.reciprocal(out=rc[:], in_=obank[:, T:2 * T]), reads=[ok], writes=["rc"])
                    add("dve", lambda e, h=h: e.tensor_tensor(out=yT[:, 4 + h, :], in0=obank[:, 0:T], in1=rc[:], op=ALU.mult),
                        reads=[ok, "rc"], writes=[("yT", 4 + h)])

            emit_S(0)
            for k in range(len(steps)):
                if k + 1 < len(steps):
                    emit_S(k + 1)
                emit_exp(k)
                emit_PV(k)
                if steps[k][1] == i:
                    yield ("head", steps[k][0])

            gate_chunks = [0, 2, 12, 14, 4, 6, 8, 10]
            for gi, ch0 in enumerate(gate_chunks):
                sk, view, _ = next_slab()
                bk, bank = next_pb()
                proj_fm("hT", sk, view, 2, bank, bk)
                sg = sgt[gi % 2]
                sgk = "sgt%d" % (gi % 2)
                add("act", lambda e, sg=sg, bank=bank: e.activation(out=sg[:, :, :], in_=bank[:, :].rearrange("p (c t) -> p c t", c=2), func=AF.Silu),
                    reads=[bk], writes=[sgk])
                add("dve", lambda e, sg=sg, ch0=ch0: e.tensor_tensor(out=yT[:, ch0:ch0 + 2, :], in0=yT[:, ch0:ch0 + 2, :], in1=sg[:, :, :], op=ALU.mult),
                    reads=[sgk, ("yT", ch0), ("yT", ch0 + 1)], writes=[("yT", ch0), ("yT", ch0 + 1)])

            yield "gates"
            if DEBUG and l == 0:
                add("sp", lambda e: e.dma_start(out=dbg_d[s, i], in_=yT[:]), reads=[("yT", c) for c in range(16)], dma="dbg")
            ensure_ln("post", l)
            for st in range(2):
                if "nospill" in VAR:
                    break
                r0 = t0 + st * 128
                if l == 0:
                    add("pool", lambda e: e.dma_start(out=hs[st][:], in_=h0buf_d[s, r0:r0 + 128, :]), reads=[("h0", s, r0)], writes=["hs%d" % st], dma="ld%d" % st)
                else:
                    add("pool", lambda e: e.dma_start(out=hs[st][:], in_=hbuf_d[s, r0:r0 + 128, :]), reads=[("hbuf", s, r0)], writes=["hs%d" % st], dma="ld%d" % st)
            yall = [("yT", c) for c in range(16)]
            for cb in range(8):
                sk, view, _ = next_slab()
                bk, bank = next_pb()
                for st in range(2):
                    for e_ in range(16):
                        add("pe", lambda e, st=st, e_=e_, bank=bank, view=view: e.matmul(
                            bank[:, st * 256:(st + 1) * 256], lhsT=yT[:, e_, st * 128:(st + 1) * 128], rhs=view[:, e_, :], start=(e_ == 0), stop=False),
                            reads=yall + [sk], writes=[bk])
                    add("pe", lambda e, st=st, cb=cb, bank=bank: e.matmul(
                        bank[:, st * 256:(st + 1) * 256], lhsT=sel33[:, :], rhs=bo[:, cb * 256:(cb + 1) * 256], start=False, stop=True),
                        reads=["sel33", "bo"], writes=[bk])
                for st in range(2):
                    hk = "hs%d" % st
                    add("dve", lambda e, st=st, cb=cb, bank=bank: e.scalar_tensor_tensor(
                        out=hs[st][:, cb * 256:(cb + 1) * 256], in0=hs[st][:, cb * 256:(cb + 1) * 256], scalar=float(ALPHA),
                        in1=bank[:, st * 256:(st + 1) * 256], op0=ALU.mult, op1=ALU.add), reads=[hk, bk], writes=[hk])
            yield "dmm"
            for st in range(2):
                if st == 1:
                    yield "tail0"
                hk = "hs%d" % st
                r0 = t0 + st * 128
                ln_rows(hs[st], hk, 4, 512)
                add("dve", lambda e, st=st: e.scalar_tensor_tensor(out=hs[st][:], in0=hs[st][:], scalar=mv[:, 0:1], in1=lng[:],
                                                                 op0=ALU.subtract, op1=ALU.mult), reads=[hk, "mv", "lng"], writes=[hk])
                add("dve", lambda e, st=st: e.scalar_tensor_tensor(out=hs[st][:], in0=hs[st][:], scalar=rs1[:, 0:1], in1=lnb[:],
                                                                 op0=ALU.mult, op1=ALU.add), reads=[hk, "rs1", "lnb"], writes=[hk])
                if l < L - 1:
                    add("pool", lambda e, st=st, r0=r0: e.dma_start(out=hbuf_d[s, r0:r0 + 128, :], in_=hs[st][:]),
                        reads=[hk], writes=[("hbuf", s, r0)], dma="st%d" % st)
                else:
                    add("pool", lambda e, st=st, r0=r0: e.dma_start(out=out_d[s, r0:r0 + 128, :], in_=hs[st][:]),
                        reads=[hk], writes=[("out", s, r0)], dma="st%d" % st)

        stage0a(*order[0], 0)
        cur_layer[0] = 0
        layer_setup(0)
        rope_chunk(order[0][0], order[0][2])
        stage0a(*order[0], 1)
        stage0b(*order[0])
        conv_layers([0])
        for l_ in range(1, L):
            pending_conv.extend((l_, si) for si in range(NSLAB))
        prev_tail = None
        for n_, (s, l, i) in enumerate(order):
            nxt = order[n_ + 1] if n_ + 1 < len(order) else None
            g_ = tile_gen(s, l, i)
            next(g_)
            next(g_)
            next(g_)
            while True:
                r = next(g_)
                if r == "gates":
                    break
                h = r[1]
                if h == 0 and prev_tail is not None:
                    next(prev_tail)
                elif h == 1:
                    if prev_tail is not None:
                        for _ in prev_tail:
                            pass
                    if nxt is not None:
                        stage0a(*nxt, 0)
                elif h == 3 and nxt is not None:
                    stage0a(*nxt, 1, (0,))
                elif h == 5 and nxt is not None:
                    stage0a(*nxt, 1, (1,))
                elif h == 6 and nxt is not None and nxt[1] == 0:
                    rope_chunk(nxt[0], nxt[2])
            if nxt is not None:
                stage0b(*nxt)
            next(g_)
            prev_tail = g_
        for _ in prev_tail:
            pass

        fin = add("sp", None)
        for op in S_.ops:
            if op["dma"] in ("st0", "st1"):
                S_.ops[fin]["deps"].add(op["idx"])
        S_.emit(nc, ctx)
    return nc


def host_consts(L, q_norm_g, kv_norm_g, pool_scale, sgu_norm_g, sgu_norm_b):
    pp = np.zeros((128, L, 20), np.float32)
    pp[:, :, 0:4] = q_norm_g.reshape(L, 4, 128).transpose(2, 0, 1)
    pp[:, :, 4:6] = kv_norm_g.reshape(L, 2, 128).transpose(2, 0, 1)
    pp[:, :, 6:10] = pool_scale.reshape(L, 4, 128).transpose(2, 0, 1)
    pp[:, :, 10:14] = sgu_norm_g.reshape(L, 4, 128).transpose(2, 0, 1)
    pp[:, :, 14:18] = sgu_norm_b.reshape(L, 4, 128).transpose(2, 0, 1)
    ident = np.eye(128, dtype=np.float32)
    mask = (np.arange(128)[:, None] <= np.arange(128)[None, :]).astype(np.float32)
    invc = np.zeros((4, 16), np.float32)
    for g in range(4):
        w = 2 << g
        invc[g] = 1.0 / np.minimum(np.arange(1, 17), w).astype(np.float32)
    half = 32
    inv_freq = (np.float32(10000.0) ** (-np.arange(half, dtype=np.float32) / np.float32(half))).astype(np.float32)
    crope = np.zeros((64, 4), np.float32)
    crope[:32, 0] = inv_freq
    crope[32:, 0] = inv_freq
    crope[:32, 1] = -1.0
    crope[32:, 1] = 1.0
    return pp, ident, mask, invc.reshape(64), crope


_CACHE = {}


def run(inputs, NSEQ, S, L, ncores):
    key = (NSEQ, S, L)
    if key not in _CACHE:
        _CACHE[key] = build_program(NSEQ, S, L)
    nc = _CACHE[key]
    f = lambda a: np.ascontiguousarray(np.asarray(a, dtype=np.float32))
    pp, ident, mask, invc, crope = host_consts(L, f(inputs["q_norm_g"]), f(inputs["kv_norm_g"]), f(inputs["pool_scale"]),
                                               f(inputs["sgu_norm_g"]), f(inputs["sgu_norm_b"]))
    x = f(inputs["x"])
    pos = np.ascontiguousarray(np.asarray(inputs["positions"], dtype=np.int32))
    shared = dict(
        ln_in_g=f(inputs["ln_in_g"]), ln_in_b=f(inputs["ln_in_b"]), w_in=f(inputs["w_in"]), pool_w=f(inputs["pool_w"]),
        w_uq=f(inputs["w_uq"]), w_ukv=f(inputs["w_ukv"]),
        sgu_wT=np.ascontiguousarray(f(inputs["sgu_w"]).transpose(0, 1, 3, 2)),
        sgu_b=f(inputs["sgu_b"]).reshape(L, 512), w_out=f(inputs["w_out"]), b_out=f(inputs["b_out"]),
        ln_post_g=f(inputs["ln_post_g"]), ln_post_b=f(inputs["ln_post_b"]),
        pparams=pp, c_ident=ident, c_mask=mask, c_invc=invc, c_rope=crope)
    in_maps = []
    for c in range(ncores):
        m = dict(shared)
        m["x"] = np.ascontiguousarray(x[c * NSEQ:(c + 1) * NSEQ])
        m["positions"] = np.ascontiguousarray(pos[c * NSEQ:(c + 1) * NSEQ])
        in_maps.append(m)
    res = run_bass_kernel_spmd(nc, in_maps, core_ids=list(range(ncores)))
    if DEBUG:
        global _DBG
        _DBG = [np.asarray(r["dbg"]) for r in res.results]
    return np.concatenate([np.asarray(r["out"], dtype=np.float32) for r in res.results], axis=0)


def kernel(**inputs):
    x = np.asarray(inputs["x"])
    B, S, _ = x.shape
    L = np.asarray(inputs["w_in"]).shape[0]
    ncores = 8
    return run(inputs, B // ncores, S, L, ncores)
```

```python
import math
import types
from contextlib import ExitStack
import numpy as np
import concourse.bass as bass
import concourse.mybir as mybir
from concourse.bass_utils import run_bass_kernel_spmd

F32 = mybir.dt.float32
BF16 = mybir.dt.bfloat16
I32 = mybir.dt.int32
AF = mybir.ActivationFunctionType
ALU = mybir.AluOpType

D = 2048
NCOL = 4416
T = 256
EPS = 1e-5
NSLOT = 5
DEBUG = False
import os
VAR = "nodefer"
SLOT_ELEMS = 4096


def _freeze(fn):
    if fn is None or fn.__closure__ is None:
        return fn
    cells = []
    for c in fn.__closure__:
        try:
            cells.append(types.CellType(c.cell_contents))
        except ValueError:
            cells.append(c)
    return types.FunctionType(fn.__code__, fn.__globals__, fn.__name__, fn.__defaults__, tuple(cells))


class Sched:
    ENG = ("pe", "act", "dve", "pool", "sp")

    def __init__(self):
        self.ops = []
        self.last_w = {}
        self.readers = {}
        self.dma_last = {}

    ALIAS = {"cqT": ["R4a", "R4b"], "cvf": ["R4a", "R4b"], "sgt0": ["R4a"], "sgt1": ["R4b"],
             "qn": ["Qa", "Qb"], "cu": ["Qa"], "cvn": ["Qb"],
             "cqn": ["Pa", "Pb"], "PT0": ["Pa"], "PT1": ["Pb"], "rq": ["Rr"], "rc": ["Rr"]}

    def add(self, eng, fn, reads=(), writes=(), dma=None):
        idx = len(self.ops)
        reads = [k for r in reads for k in self.ALIAS.get(r, [r])]
        writes = [k for r in writes for k in self.ALIAS.get(r, [r])]
        deps = set()
        for r in reads:
            w = self.last_w.get(r)
            if w is not None:
                deps.add(w)
        for wk in writes:
            w = self.last_w.get(wk)
            if w is not None:
                deps.add(w)
            lastr = {}
            for rd in self.readers.get(wk, ()):
                o = self.ops[rd]
                if o["dma"] is not None:
                    deps.add(rd)
                else:
                    lastr[o["eng"]] = rd
            deps.update(lastr.values())
        if dma is not None:
            p = self.dma_last.get(dma)
            if p is not None:
                deps.add(p)
            self.dma_last[dma] = idx
        deps.discard(idx)
        for r in reads:
            self.readers.setdefault(r, []).append(idx)
        for wk in writes:
            self.last_w[wk] = idx
            self.readers[wk] = []
        self.ops.append(dict(eng=eng, fn=_freeze(fn), deps=deps, dma=dma, idx=idx, signal=False))
        return idx

    def emit(self, nc, ctx):
        ops = self.ops
        for op in ops:
            for d in op["deps"]:
                p = ops[d]
                if p["dma"] is None and p["eng"] == "pe" and op["eng"] == "pe" and op["dma"] is None:
                    continue
                p["signal"] = True
        cnt = {}
        for op in ops:
            if op["dma"] is not None:
                k = ("dma", op["dma"])
                cnt[k] = cnt.get(k, 0) + 16
                op["sem"] = k
                op["val"] = cnt[k]
            elif op["signal"]:
                k = ("eng", op["eng"])
                cnt[k] = cnt.get(k, 0) + 1
                op["sem"] = k
                op["val"] = cnt[k]
        sems = {}
        for n, k in enumerate(cnt):
            sems[k] = ctx.enter_context(nc.semaphore("sem%d" % n))
        by_eng = {e: [o for o in ops if o["eng"] == e] for e in self.ENG}
        block = ctx.enter_context(nc.Block())

        def run(engh, lst):
            waited = {}
            for op in lst:
                need = {}
                for d in op["deps"]:
                    p = ops[d]
                    if "sem" not in p:
                        continue
                    need[p["sem"]] = max(need.get(p["sem"], 0), p["val"])
                for k, v in need.items():
                    if waited.get(k, 0) >= v:
                        continue
                    engh.wait_ge(sems[k], v)
                    waited[k] = v
                if op["fn"] is None:
                    continue
                ins = op["fn"](engh)
                if "sem" in op:
                    ins.then_inc(sems[op["sem"]], 16 if op["dma"] is not None else 1)

        @block.tensor
        def _(e):
            run(e, by_eng["pe"])

        @block.scalar
        def _(e):
            run(e, by_eng["act"])

        @block.vector
        def _(e):
            run(e, by_eng["dve"])

        @block.gpsimd
        def _(e):
            run(e, by_eng["pool"])

        @block.sync
        def _(e):
            run(e, by_eng["sp"])


def slab_plan():
    pl = []
    pl += [("in", 1024, 256, 16), ("in", 1280, 256, 16), ("in", 1536, 256, 16), ("in", 1792, 64, 16)]
    pl += [("in", 3392, 256, 16), ("in", 3648, 256, 16)]
    pl += [("in", 0, 256, 16), ("in", 256, 256, 16)]
    pl += [("in", 2880, 256, 16), ("in", 3136, 256, 16)]
    pl += [("uq", 0, 768, 4), ("uq", 768, 768, 4), ("ukv", 0, 2048, 2)]
    pl += [("in", 512, 256, 16), ("in", 768, 256, 16)]
    pl += [("in", 3904, 256, 16), ("in", 4160, 256, 16)]
    pl += [("in", 1856 + 256 * i, 256, 16) for i in range(4)]
    pl += [("out", 256 * i, 256, 16) for i in range(8)]
    return pl


PLAN = slab_plan()
NSLAB = len(PLAN)


def build_program(NSEQ, S, L):
    NT = S // T
    NKT = S // 128
    ALPHA = (2.0 * L) ** 0.25
    SCALE = 192 ** -0.5
    nc = bass.Bass("TRN2", target_bir_lowering=False)
    dt_in = lambda name, shape, dt=F32: nc.dram_tensor(name, shape, dt, kind="ExternalInput").ap()
    x_d = dt_in("x", [NSEQ, S, D])
    pos_d = dt_in("positions", [NSEQ, S], I32)
    lnin_g = dt_in("ln_in_g", [D])
    lnin_b = dt_in("ln_in_b", [D])
    w_in = dt_in("w_in", [L, D, NCOL])
    pool_w = dt_in("pool_w", [L, 4, 128, 128])
    w_uq = dt_in("w_uq", [L, 512, 1536])
    w_ukv = dt_in("w_ukv", [L, 256, 2048])
    sgu_wT = dt_in("sgu_wT", [L, 4, 128, 128])
    sgu_bias = dt_in("sgu_b", [L, 512])
    w_out = dt_in("w_out", [L, D, D])
    b_out = dt_in("b_out", [L, D])
    lnp_g = dt_in("ln_post_g", [L, D])
    lnp_b = dt_in("ln_post_b", [L, D])
    pp_d = dt_in("pparams", [128, L, 20])
    c_ident = dt_in("c_ident", [128, 128])
    c_mask = dt_in("c_mask", [128, 128])
    c_invc = dt_in("c_invc", [64])
    c_rope = dt_in("c_rope", [64, 4])
    out_d = nc.dram_tensor("out", [NSEQ, S, D], F32, kind="ExternalOutput").ap()
    wb_d = nc.dram_tensor("wb", [L, NSLAB, 128, SLOT_ELEMS], BF16, kind="Internal").ap()
    hbuf_d = nc.dram_tensor("hbuf", [NSEQ, S, D], F32, kind="Internal").ap()
    h0buf_d = nc.dram_tensor("h0buf", [NSEQ, S, D], F32, kind="Internal").ap()
    rope_d = nc.dram_tensor("ropetab", [NSEQ, 2, 64, S], F32, kind="Internal").ap()

    dbg_d = nc.dram_tensor("dbg", [NSEQ, NT, 128, 16, T], BF16, kind="ExternalOutput").ap() if DEBUG else None
    S_ = Sched()
    add = S_.add
    with ExitStack() as ctx:
        sb = lambda name, shape, dt: ctx.enter_context(nc.sbuf_tensor(name, shape, dt))
        Kc = sb("Kc", [128, 8, S], BF16)
        Vc = sb("Vc", [128, NKT, 1024], BF16)
        KRc = sb("KRc", [128, S], BF16)
        hs = [sb("hs%d" % i, [128, D], F32) for i in range(2)]
        hT = sb("hT", [128, 16, T], BF16)
        yT = sb("yT", [128, 16, T], BF16)
        R4 = sb("R4", [128, 4 * T], F32)
        cqT = R4[:, :].rearrange("p (c t) -> p c t", c=4)
        cqn = sb("cqn", [128, 4, T], BF16)
        ckvT = sb("ckvT", [128, 2, T], F32)
        ckvn = sb("ckvn", [128, 2, T], BF16)
        rq = sb("rq", [128, T], F32)
        rkv = sb("rkv", [128, T], F32)
        qn = sb("qn", [128, 8, T], BF16)
        qr = sb("qr", [128, 8, T], BF16)
        cs2 = sb("cs2", [64, T], F32)
        sn2 = sb("sn2", [64, T], F32)
        rt = [sb("rt%d" % i, [64, T], F32) for i in range(4)]
        rti = sb("rti", [64, T], I32)
        axT = sb("axT", [128, 4, 16 + T], F32)
        tA = sb("tA", [128, 16 + T], F32)
        tB = sb("tB", [128, 16 + T], F32)
        pl = [sb("pl%d" % i, [128, T], BF16) for i in range(4)]
        cu = qn[:, 0:4, :]
        cvf = R4[:, :].rearrange("p (s n) -> p s n", s=2)
        cvn = qn[:, 4:8, :].rearrange("p a t -> p (a t)").rearrange("p (s n) -> p s n", s=2)
        PT = [cqn[:, 2 * i:2 * i + 2, :] for i in range(2)]
        rc = rq
        sgt = [R4[:, i * 2 * T:(i + 1) * 2 * T].rearrange("p (c t) -> p c t", c=2) for i in range(2)]
        stmp = sb("stmp", [128, 512], F32)
        stmp2 = sb("stmp2", [128, 512], F32)
        slots = [sb("slot%d" % i, [128, SLOT_ELEMS], BF16) for i in range(NSLOT)]
        lng = sb("lng", [128, D], F32)
        lnb = sb("lnb", [128, D], F32)
        identf = sb("identf", [128, 128], F32)
        ones_bf = sb("ones_bf", [128, 128], BF16)
        maskf = sb("maskf", [128, 128], F32)
        mask_bf = sb("mask_bf", [128, 128], BF16)
        epsb = sb("epsb", [128, 1], F32)
        nhalf = sb("nhalf", [128, 1], F32)
        invc = sb("invc", [128, 64], F32)
        pp = sb("pp", [128, L, 20], F32)
        crope = sb("crope", [64, 4], F32)
        bsb = sb("bsb", [128, 512], F32)
        rsw = sb("rsw", [128, 512], F32)
        wpool = sb("wpool", [128, 4, 128], BF16)
        wsT = sb("wsT", [128, 4, 128], BF16)
        bo = sb("bo", [128, D], BF16)
        sel33 = sb("sel33", [128, 128], BF16)
        stats = sb("stats", [128, 24], F32)
        mv = sb("mv", [128, 2], F32)
        rs1 = sb("rs1", [128, 1], F32)
        banks = [ctx.enter_context(nc.psum_tensor("bank%d" % i, [128, 512], F32)) for i in range(8)]
        PB = [("P0", banks[0]), ("P1", banks[1]), ("X0", banks[6])]
        SB_ = [("S0", banks[2]), ("S1", banks[3])]
        OB = [("O0", banks[4]), ("O1", banks[5])]
        X0 = ("X0", banks[6])
        X1 = ("X1", banks[7])


        def conv_src(l, kind, c0, ncols, kc):
            if kind == "in":
                return w_in[l].rearrange("(k p) n -> p k n", p=128)[:, :, c0:c0 + ncols]
            if kind == "uq":
                return w_uq[l].rearrange("(k p) n -> p k n", p=128)[:, :, c0:c0 + ncols]
            if kind == "ukv":
                return w_ukv[l].rearrange("(k p) n -> p k n", p=128)
            return w_out[l].rearrange("(k p) n -> p k n", p=128)[:, :, c0:c0 + ncols]

        nconv = [0]

        def conv_one(l, si):
            kind, c0, ncols, kc = PLAN[si]
            dst = wb_d[l, si, :, 0:kc * ncols].rearrange("p (k n) -> p k n", k=kc)
            src = conv_src(l, kind, c0, ncols, kc)
            add("pool", lambda e: e.dma_start(out=dst, in_=src), writes=[("wb", l, si)], dma="cv%d" % (nconv[0] % 6))
            nconv[0] += 1

        def conv_layers(ls):
            for l in ls:
                for si in range(NSLAB):
                    conv_one(l, si)

        pending_conv = []

        add("sp", lambda e: e.dma_start(out=identf[:], in_=c_ident[:, :]), writes=["identf"], dma="c0")
        add("sp", lambda e: e.dma_start(out=maskf[:], in_=c_mask[:, :]), writes=["maskf"], dma="c1")
        add("sp", lambda e: e.dma_start(out=invc[:], in_=c_invc.partition_broadcast(128)), writes=["invc"], dma="c2")
        add("sp", lambda e: e.dma_start(out=pp[:], in_=pp_d[:, :, :]), writes=["pp"], dma="c3")
        add("sp", lambda e: e.dma_start(out=crope[:], in_=c_rope[:, :]), writes=["crope"], dma="c4")
        add("dve", lambda e: e.tensor_copy(out=mask_bf[:], in_=maskf[:]), reads=["maskf"], writes=["mask_bf"])
        add("dve", lambda e: e.memset(ones_bf[:], 1.0), writes=["ones_bf"])
        add("dve", lambda e: e.memset(epsb[:], float(EPS)), writes=["epsb"])
        add("dve", lambda e: e.memset(nhalf[:], -0.5), writes=["nhalf"])
        add("dve", lambda e: e.memset(sel33[:], 0.0), writes=["sel33"])
        add("dve", lambda e: e.memset(sel33[0:1, :], 1.0), writes=["sel33"])
        add("dve", lambda e: e.memset(sel33[32:33, :], 1.0), writes=["sel33"])
        add("dve", lambda e: e.memset(bo[:], 0.0), writes=["bo"])
        add("dve", lambda e: e.memset(KRc[64:128, :], 0.0), writes=["KRc"])
        add("dve", lambda e: e.memset(qr[64:128, :, :], 0.0), writes=["qr"])

        TWO_PI = 2.0 * math.pi
        C1 = 6.28125
        C2 = TWO_PI - C1
        def rope_chunk(s, ch):
            if True:
                c0 = ch * T
                add("pool", lambda e, s=s, c0=c0: e.dma_start(out=rti[:], in_=pos_d[s, c0:c0 + T].partition_broadcast(64)),
                    writes=["rti"], dma="rp")
                add("dve", lambda e: e.tensor_copy(out=rt[0][:], in_=rti[:]), reads=["rti"], writes=["rt0"])
                add("dve", lambda e: e.tensor_scalar(out=rt[0][:], in0=rt[0][:], scalar1=crope[:, 0:1], scalar2=None, op0=ALU.mult),
                    reads=["rt0", "crope"], writes=["rt0"])
                for which in range(2):
                    off = math.pi / 2 if which == 0 else 0.0
                    add("dve", lambda e, off=off: e.tensor_scalar(out=rt[1][:], in0=rt[0][:], scalar1=float(off), scalar2=None, op0=ALU.add),
                        reads=["rt0"], writes=["rt1"])
                    add("dve", lambda e: e.tensor_scalar(out=rt[2][:], in0=rt[1][:], scalar1=float(1.0 / TWO_PI), scalar2=None, op0=ALU.mult),
                        reads=["rt1"], writes=["rt2"])
                    add("dve", lambda e: e.tensor_copy(out=rti[:], in_=rt[2][:]), reads=["rt2"], writes=["rti"])
                    add("dve", lambda e: e.tensor_copy(out=rt[2][:], in_=rti[:]), reads=["rti"], writes=["rt2"])
                    add("dve", lambda e: e.scalar_tensor_tensor(out=rt[1][:], in0=rt[2][:], scalar=float(-C1), in1=rt[1][:], op0=ALU.mult, op1=ALU.add),
                        reads=["rt2", "rt1"], writes=["rt1"])
                    add("dve", lambda e: e.scalar_tensor_tensor(out=rt[1][:], in0=rt[2][:], scalar=float(-C2), in1=rt[1][:], op0=ALU.mult, op1=ALU.add),
                        reads=["rt2", "rt1"], writes=["rt1"])
                    add("dve", lambda e: e.tensor_scalar(out=rt[1][:], in0=rt[1][:], scalar1=float(math.pi), scalar2=float(-math.pi), op0=ALU.min, op1=ALU.max),
                        reads=["rt1"], writes=["rt1"])
                    add("act", lambda e: e.activation(out=rt[3][:], in_=rt[1][:], func=AF.Sin), reads=["rt1"], writes=["rt3"])
                    if which == 1:
                        add("dve", lambda e: e.tensor_scalar(out=rt[3][:], in0=rt[3][:], scalar1=crope[:, 1:2], scalar2=None, op0=ALU.mult),
                            reads=["rt3", "crope"], writes=["rt3"])
                    add("pool", lambda e, s=s, which=which, c0=c0: e.dma_start(out=rope_d[s, which, :, c0:c0 + T], in_=rt[3][:]),
                        reads=["rt3"], writes=[("rope", s, ch)], dma="rp")

        stream = []
        order = [(s, l, i) for s in range(NSEQ) for l in range(L) for i in range(NT)]
        for (s, l, i) in order:
            for si in range(NSLAB):
                stream.append((l, si))
        st_state = dict(issued=0, cur=0)

        def issue_loads(upto):
            while st_state["issued"] < min(upto, len(stream)):
                n = st_state["issued"]
                l, si = stream[n]
                slot = n % NSLOT
                kind, c0, ncols, kc = PLAN[si]
                ne = kc * ncols
                add("sp", lambda e, l=l, si=si, slot=slot, ne=ne: e.dma_start(out=slots[slot][:, 0:ne], in_=wb_d[l, si, :, 0:ne]),
                    reads=[("wb", l, si)], writes=["slot%d" % slot], dma="slot%d" % slot)
                st_state["issued"] += 1

        def next_slab():
            if pending_conv:
                conv_one(*pending_conv.pop(0))
            n = st_state["cur"]
            issue_loads(n + NSLOT)
            st_state["cur"] += 1
            slot = n % NSLOT
            l, si = stream[n]
            kind, c0, ncols, kc = PLAN[si]
            view = slots[slot][:, 0:kc * ncols].rearrange("p (k n) -> p k n", k=kc)
            return "slot%d" % slot, view, slots[slot]

        pbi = [0]

        def next_pb():
            r = PB[pbi[0] % 3]
            pbi[0] += 1
            return r

        ln_state = [None]

        def ensure_ln(which, l):
            key = (which, l)
            if ln_state[0] == key:
                return
            ln_state[0] = key
            gsrc = lnin_g if which == "in" else lnp_g[l]
            bsrc = lnin_b if which == "in" else lnp_b[l]
            add("pool", lambda e: e.dma_start(out=lng[:], in_=gsrc.partition_broadcast(128)), writes=["lng"], dma="lng")
            add("pool", lambda e: e.dma_start(out=lnb[:], in_=bsrc.partition_broadcast(128)), writes=["lnb"], dma="lnb")

        def layer_setup(l):
            add("pool", lambda e: e.dma_start(out=wpool[:], in_=pool_w[l].rearrange("g c d -> c g d")), writes=["wpool"], dma="wpool")
            add("pool", lambda e: e.dma_start(out=wsT[:], in_=sgu_wT[l].rearrange("h s t -> s h t")), writes=["wsT"], dma="wsT")
            for h in range(4):
                add("dve", lambda e, h=h: e.tensor_tensor(out=wsT[:, h, :], in0=wsT[:, h, :], in1=mask_bf[:], op=ALU.mult),
                    reads=["wsT", "mask_bf"], writes=["wsT"])
            add("pool", lambda e: e.dma_start(out=bsb[:], in_=sgu_bias[l].partition_broadcast(128)), writes=["bsb"], dma="bsb")
            add("pe", lambda e: e.matmul(X1[1][:, :], lhsT=ones_bf[:], rhs=wsT[:].rearrange("p h t -> p (h t)"), start=True, stop=True),
                reads=["wsT", "ones_bf"], writes=["X1"])
            add("dve", lambda e: e.tensor_copy(out=rsw[:], in_=X1[1][:, :]), reads=["X1"], writes=["rsw"])
            for hd in range(4):
                add("dve", lambda e, hd=hd: e.scalar_tensor_tensor(out=rsw[:, hd * 128:(hd + 1) * 128], in0=rsw[:, hd * 128:(hd + 1) * 128],
                                                                 scalar=pp[:, l, 14 + hd:15 + hd], in1=bsb[:, hd * 128:(hd + 1) * 128],
                                                                 op0=ALU.mult, op1=ALU.add), reads=["rsw", "bsb", "pp"], writes=["rsw"])
            for q4 in range(4):
                add("pool", lambda e, q4=q4: e.dma_start(out=stmp[0:1, :], in_=b_out[l, q4 * 512:(q4 + 1) * 512].rearrange("(a n) -> a n", a=1)),
                    writes=["stmp"], dma="bo_a")
                add("pool", lambda e, q4=q4: e.dma_start(out=stmp[32:33, :], in_=b_out[l, q4 * 512:(q4 + 1) * 512].rearrange("(a n) -> a n", a=1)),
                    writes=["stmp"], dma="bo_b")
                sl = slice(q4 * 512, (q4 + 1) * 512)
                add("dve", lambda e, sl=sl: e.tensor_copy(out=bo[0:1, sl], in_=stmp[0:1, :]), reads=["stmp"], writes=["bo"])
                add("dve", lambda e, sl=sl: e.tensor_copy(out=bo[32:33, sl], in_=stmp[32:33, :]), reads=["stmp"], writes=["bo"])
                add("dve", lambda e, sl=sl: e.tensor_tensor(out=stmp[32:33, :], in0=stmp[32:33, :], in1=bo[32:33, sl], op=ALU.subtract),
                    reads=["stmp", "bo"], writes=["stmp"])
                add("dve", lambda e, sl=sl: e.tensor_copy(out=bo[32:33, sl], in_=stmp[32:33, :]), reads=["stmp"], writes=["bo"])

        def ln_rows(buf, key, nch, width, use_pool=True):
            for c in range(nch):
                add("dve", lambda e, c=c: e.bn_stats(out=stats[:, c * 6:(c + 1) * 6], in_=buf[:, c * width:(c + 1) * width]),
                    reads=[key], writes=["stats"])
            add("dve", lambda e: e.bn_aggr(out=mv[:], in_=stats[:, 0:nch * 6]), reads=["stats"], writes=["mv"])
            if use_pool:
                add("dve", lambda e: e.tensor_scalar(out=rs1[:], in0=mv[:, 1:2], scalar1=float(EPS), scalar2=None, op0=ALU.add), reads=["mv"], writes=["rs1"])
                add("pool", lambda e: e.tensor_tensor(out=rs1[:], in0=rs1[:], in1=nhalf[:, 0:1], op=ALU.pow), reads=["rs1", "nhalf"], writes=["rs1"])
            else:
                add("act", lambda e: e.activation(out=rs1[:], in_=mv[:, 1:2], func=AF.Sqrt, bias=epsb[:, 0:1]), reads=["mv", "epsb"], writes=["rs1"])
                add("dve", lambda e: e.reciprocal(out=rs1[:], in_=rs1[:]), reads=["rs1"], writes=["rs1"])

        def proj_fm(hkey, slabkey, view, nblk, bank, bkey, M=128):
            for blk in range(nblk):
                for kc in range(16):
                    add("pe", lambda e, blk=blk, kc=kc: e.matmul(
                        bank[0:M, blk * T:(blk + 1) * T], lhsT=view[:, kc, blk * M:(blk + 1) * M], rhs=hT[:, kc, :],
                        start=(kc == 0), stop=(kc == 15)),
                        reads=[hkey, slabkey], writes=[bkey])

        cur_layer = [None]

        def stage0a(s, l, i, part, sts=(0, 1)):
            t0 = i * T
            if part == 1 and l == 0:
                ensure_ln("in", 0)
            for st in sts:
                r0 = t0 + st * 128
                src = x_d[s, r0:r0 + 128, :] if l == 0 else hbuf_d[s, r0:r0 + 128, :]
                hk = "hs%d" % st
                rd = [("hbuf", s, r0)] if l > 0 else []
                if part == 0:
                    add("pool", lambda e: e.dma_start(out=hs[st][:], in_=src), reads=rd, writes=[hk], dma="ld%d" % st)
                elif l == 0:
                    ln_rows(hs[st], hk, 4, 512)
                    add("dve", lambda e: e.scalar_tensor_tensor(out=hs[st][:], in0=hs[st][:], scalar=mv[:, 0:1], in1=lng[:],
                                                              op0=ALU.subtract, op1=ALU.mult), reads=[hk, "mv", "lng"], writes=[hk])
                    add("dve", lambda e: e.scalar_tensor_tensor(out=hs[st][:], in0=hs[st][:], scalar=rs1[:, 0:1], in1=lnb[:],
                                                              op0=ALU.mult, op1=ALU.add), reads=[hk, "rs1", "lnb"], writes=[hk])
                    add("pool", lambda e: e.dma_start(out=h0buf_d[s, r0:r0 + 128, :], in_=hs[st][:]),
                        reads=[hk], writes=[("h0", s, r0)], dma="h0s%d" % st)

        def stage0b(s, l, i):
            t0 = i * T
            for st in range(2):
                hk = "hs%d" % st
                for g4 in range(4):
                    bk, bank = X0 if "x0" in VAR else next_pb()
                    for c in range(4):
                        ch = g4 * 4 + c
                        add("pe", lambda e: e.transpose(out=bank[:, c * 128:(c + 1) * 128], in_=hs[st][:, ch * 128:(ch + 1) * 128],
                                                      identity=identf[:]),
                            reads=[hk, "identf"], writes=[bk])
                    add("act", lambda e: e.activation(out=hT[:, g4 * 4:(g4 + 1) * 4, st * 128:(st + 1) * 128],
                                                    in_=bank[:, :].rearrange("p (c t) -> p c t", c=4), func=AF.Copy),
                        reads=[bk], writes=["hT"])
            add("pool", lambda e: e.dma_start(out=cs2[:], in_=rope_d[s, 0, :, t0:t0 + T]), reads=[("rope", s, i)], writes=["cs2"], dma="cs2")
            add("pool", lambda e: e.dma_start(out=sn2[:], in_=rope_d[s, 1, :, t0:t0 + T]), reads=[("rope", s, i)], writes=["sn2"], dma="sn2")

        def tile_gen(s, l, i):
            t0 = i * T
            if cur_layer[0] != l:
                cur_layer[0] = l
                layer_setup(l)
            deferred = []

            for j in range(2):
                sk, view, _ = next_slab()
                bk, bank = next_pb()
                proj_fm("hT", sk, view, 2, bank, bk)
                add("act", lambda e, j=j, bank=bank: e.activation(out=cqT[:, 2 * j:2 * j + 2, :], in_=bank[:, :].rearrange("p (c t) -> p c t", c=2), func=AF.Copy),
                    reads=[bk], writes=["cqT"])
                add("dve", lambda e, j=j, bank=bank: e.tensor_tensor(out=cqn[:, 2 * j:2 * j + 2, :], in0=bank[:, :].rearrange("p (c t) -> p c t", c=2),
                                                                    in1=cqT[:, 2 * j:2 * j + 2, :], op=ALU.mult),
                    reads=[bk, "cqT"], writes=["cqn"])
            for c in range(4):
                add("pe", lambda e, c=c: e.matmul(X1[1][:, 0:T], lhsT=ones_bf[:], rhs=cqn[:, c, :], start=(c == 0), stop=(c == 3)),
                    reads=["cqn", "ones_bf"], writes=["X1"])
            add("act", lambda e: e.activation(out=rq[:], in_=X1[1][:, 0:T], func=AF.Sqrt, scale=1.0 / 512, bias=epsb[:, 0:1]), reads=["X1", "epsb"], writes=["rq"])
            add("dve", lambda e: e.reciprocal(out=rq[:], in_=rq[:]), reads=["rq"], writes=["rq"])
            for c in range(4):
                add("dve", lambda e, c=c: e.scalar_tensor_tensor(out=cqn[:, c, :], in0=cqT[:, c, :], scalar=pp[:, l, c:c + 1], in1=rq[:],
                                                              op0=ALU.mult, op1=ALU.mult), reads=["cqT", "rq", "pp"], writes=["cqn"])
            sk, view, _ = next_slab()
            bk, bank = next_pb()
            proj_fm("hT", sk, view, 2, bank, bk)
            add("act", lambda e, bank=bank: e.activation(out=ckvT[:, :, :], in_=bank[:, :].rearrange("p (c t) -> p c t", c=2), func=AF.Copy),
                reads=[bk], writes=["ckvT"])
            add("dve", lambda e, bank=bank: e.tensor_tensor(out=ckvn[:, :, :], in0=bank[:, :].rearrange("p (c t) -> p c t", c=2), in1=ckvT[:, :, :], op=ALU.mult),
                reads=[bk, "ckvT"], writes=["ckvn"])
            for c in range(2):
                add("pe", lambda e, c=c: e.matmul(X1[1][:, T:2 * T], lhsT=ones_bf[:], rhs=ckvn[:, c, :], start=(c == 0), stop=(c == 1)),
                    reads=["ckvn", "ones_bf"], writes=["X1"])
            add("act", lambda e: e.activation(out=rkv[:], in_=X1[1][:, T:2 * T], func=AF.Sqrt, scale=1.0 / 256, bias=epsb[:, 0:1]), reads=["X1", "epsb"], writes=["rkv"])
            add("dve", lambda e: e.reciprocal(out=rkv[:], in_=rkv[:]), reads=["rkv"], writes=["rkv"])
            for c in range(2):
                add("dve", lambda e, c=c: e.scalar_tensor_tensor(out=ckvn[:, c, :], in0=ckvT[:, c, :], scalar=pp[:, l, 4 + c:5 + c], in1=rkv[:],
                                                              op0=ALU.mult, op1=ALU.mult), reads=["ckvT", "rkv", "pp"], writes=["ckvn"])

            rope_ctr = [0]

            def rope_apply(bank, bkey, col0, dst, dkey):
                b0 = 2 * (rope_ctr[0] % 2)
                rope_ctr[0] += 1
                ra, rb = rt[b0], rt[b0 + 1]
                ka, kb = "rt%d" % b0, "rt%d" % (b0 + 1)
                add("act", lambda e: e.activation(out=ra[:], in_=bank[0:64, col0:col0 + T], func=AF.Copy), reads=[bkey], writes=[ka])
                add("act", lambda e: e.activation(out=rb[0:32, :], in_=bank[32:64, col0:col0 + T], func=AF.Copy), reads=[bkey], writes=[kb])
                add("act", lambda e: e.activation(out=rb[32:64, :], in_=bank[0:32, col0:col0 + T], func=AF.Copy), reads=[bkey], writes=[kb])
                add("dve", lambda e: e.tensor_tensor(out=ra[:], in0=ra[:], in1=cs2[:], op=ALU.mult), reads=[ka, "cs2"], writes=[ka])
                add("dve", lambda e: e.tensor_tensor(out=rb[:], in0=rb[:], in1=sn2[:], op=ALU.mult), reads=[kb, "sn2"], writes=[kb])
                add("dve", lambda e: e.tensor_tensor(out=dst, in0=ra[:], in1=rb[:], op=ALU.add), reads=[ka, kb], writes=[dkey])

            sk, view, _ = next_slab()
            bk, bank = next_pb()
            for kc in range(16):
                add("pe", lambda e, kc=kc, bank=bank, view=view: e.matmul(bank[0:64, 0:T], lhsT=view[:, kc, 0:64], rhs=hT[:, kc, :], start=(kc == 0), stop=(kc == 15)),
                    reads=["hT", sk], writes=[bk])
            rope_apply(bank, bk, 0, KRc[0:64, t0:t0 + T], "KRc")

            yield "a1"
            for j in range(2):
                sk, view, _ = next_slab()
                bk, bank = next_pb()
                for st in range(2):
                    for kc in range(16):
                        add("pe", lambda e, st=st, kc=kc, bank=bank, view=view: e.matmul(
                            bank[:, st * 256:(st + 1) * 256], lhsT=hT[:, kc, st * 128:(st + 1) * 128], rhs=view[:, kc, :],
                            start=(kc == 0), stop=(kc == 15)), reads=["hT", sk], writes=[bk])
                add("act", lambda e, j=j, bank=bank: e.activation(out=cvf[:, :, j * 256:(j + 1) * 256], in_=bank[:, :].rearrange("p (s n) -> p s n", s=2), func=AF.Gelu),
                    reads=[bk], writes=["cvf"])
            for st in range(2):
                add("dve", lambda e, st=st: e.bn_stats(out=stats[:, 0:6], in_=cvf[:, st, :]), reads=["cvf"], writes=["stats"])
                add("dve", lambda e: e.bn_aggr(out=mv[:], in_=stats[:, 0:6]), reads=["stats"], writes=["mv"])
                add("act", lambda e: e.activation(out=rs1[:], in_=mv[:, 1:2], func=AF.Sqrt, bias=epsb[:, 0:1]), reads=["mv", "epsb"], writes=["rs1"])
                add("dve", lambda e: e.reciprocal(out=rs1[:], in_=rs1[:]), reads=["rs1"], writes=["rs1"])
                add("dve", lambda e, st=st: e.tensor_scalar(out=cvn[:, st, :], in0=cvf[:, st, :], scalar1=mv[:, 0:1], scalar2=rs1[:, 0:1],
                                                        op0=ALU.subtract, op1=ALU.mult), reads=["cvf", "mv", "rs1"], writes=["cvn"])
            if i == 0:
                add("dve", lambda e: e.memset(axT[:, :, 0:16], 0.0), writes=["axT"])
            else:
                add("dve", lambda e: e.tensor_copy(out=axT[:, :, 0:16], in_=axT[:, :, T:T + 16]), reads=["axT"], writes=["axT"])
            for j in range(2):
                sk, view, _ = next_slab()
                bk, bank = next_pb()
                proj_fm("hT", sk, view, 2, bank, bk)
                add("act", lambda e, j=j, bank=bank: e.activation(out=axT[:, 2 * j:2 * j + 2, 16:16 + T], in_=bank[:, :].rearrange("p (c t) -> p c t", c=2), func=AF.Copy),
                    reads=[bk], writes=["axT"])
            W = 16 + T
            for g in range(4):
                w = 2 << g
                add("dve", lambda e, g=g: e.tensor_tensor(out=tA[:, 1:W], in0=axT[:, g, 1:W], in1=axT[:, g, 0:W - 1], op=ALU.add),
                    reads=["axT"], writes=["tA"])
                cur, ck, oth, ok_ = tA, "tA", tB, "tB"
                sh = 2
                lo = 1
                while sh < w:
                    lo2 = lo + sh
                    add("dve", lambda e, cur=cur, oth=oth, sh=sh, lo2=lo2: e.tensor_tensor(out=oth[:, lo2:W], in0=cur[:, lo2:W], in1=cur[:, lo2 - sh:W - sh], op=ALU.add),
                        reads=[ck], writes=[ok_])
                    cur, ck, oth, ok_ = oth, ok_, cur, ck
                    lo = lo2
                    sh *= 2
                plb = pl[g]
                pk = "pl%d" % g
                add("dve", lambda e, cur=cur, g=g, w=w, plb=plb: e.scalar_tensor_tensor(out=plb[:], in0=cur[:, 16:W], scalar=1.0 / w, in1=axT[:, g, 16:W],
                                                                                    op0=ALU.mult, op1=ALU.subtract), reads=[ck, "axT"], writes=[pk])
                if i == 0:
                    add("dve", lambda e, cur=cur, g=g: e.tensor_tensor(out=stmp[:, 0:16], in0=cur[:, 16:32], in1=invc[:, g * 16:(g + 1) * 16], op=ALU.mult),
                        reads=[ck, "invc"], writes=["stmp"])
                    add("dve", lambda e, g=g, plb=plb: e.tensor_tensor(out=plb[:, 0:16], in0=stmp[:, 0:16], in1=axT[:, g, 16:32], op=ALU.subtract),
                        reads=["stmp", "axT", pk], writes=[pk])
                def pool_mm(g=g, plb=plb, pk=pk):
                    xk, xbank = next_pb()
                    add("pe", lambda e: e.matmul(xbank[:, 0:T], lhsT=wpool[:, g, :], rhs=plb[:], start=True, stop=True),
                        reads=[pk, "wpool"], writes=[xk])
                    add("act", lambda e: e.activation(out=yT[:, g, :], in_=xbank[:, 0:T], func=AF.Identity, scale=pp[:, l, 6 + g:7 + g]),
                        reads=[xk, "pp"], writes=[("yT", g)])
                deferred.append(pool_mm)

            for j in range(2):
                sk, view, _ = next_slab()
                bk, bank = next_pb()
                proj_fm("hT", sk, view, 2, bank, bk)
                add("act", lambda e, j=j, bank=bank: e.activation(out=cu[:, 2 * j:2 * j + 2, :], in_=bank[:, :].rearrange("p (c t) -> p c t", c=2), func=AF.Gelu),
                    reads=[bk], writes=["cu"])
            for fn_ in deferred:
                fn_()
            deferred = []
            def sgu_mm():
                for st in range(2):
                    for hd in range(4):
                        add("pe", lambda e, st=st, hd=hd: e.matmul(X1[1][:, hd * 128:(hd + 1) * 128], lhsT=cvn[:, st, hd * 128:(hd + 1) * 128], rhs=wsT[:, hd, :],
                                                               start=True, stop=True), reads=["cvn", "wsT"], writes=["X1"])
                    for hd in range(4):
                        add("dve", lambda e, hd=hd: e.scalar_tensor_tensor(out=stmp2[:, hd * 128:(hd + 1) * 128], in0=X1[1][:, hd * 128:(hd + 1) * 128],
                                                                         scalar=pp[:, l, 10 + hd:11 + hd], in1=rsw[:, hd * 128:(hd + 1) * 128],
                                                                         op0=ALU.mult, op1=ALU.add), reads=["X1", "rsw", "pp"], writes=["stmp2"])
                    add("dve", lambda e, st=st: e.tensor_tensor(out=yT[:, 12:16, st * 128:(st + 1) * 128], in0=stmp2[:, :].rearrange("p (h t) -> p h t", h=4),
                                                               in1=cu[:, :, st * 128:(st + 1) * 128], op=ALU.mult),
                        reads=["stmp2", "cu"], writes=[("yT", 12), ("yT", 13), ("yT", 14), ("yT", 15)])
            sgu_mm()

            yield "ac"
            for half in range(2):
                sk, view, _ = next_slab()
                for hp in range(2):
                    bk, bank = next_pb()
                    for hh2 in range(2):
                        hh = hp * 2 + hh2
                        for kc in range(4):
                            add("pe", lambda e, kc=kc, hh=hh, hh2=hh2, bank=bank, view=view: e.matmul(
                                bank[:, hh2 * T:(hh2 + 1) * T], lhsT=view[:, kc, hh * 192:hh * 192 + 128], rhs=cqn[:, kc, :],
                                start=(kc == 0), stop=(kc == 3)), reads=["cqn", sk], writes=[bk])
                    h0 = half * 4 + hp * 2
                    add("act", lambda e, h0=h0, bank=bank: e.activation(out=qn[:, h0:h0 + 2, :], in_=bank[:, :].rearrange("p (c t) -> p c t", c=2), func=AF.Copy),
                        reads=[bk], writes=["qn"])
                for hp in range(2):
                    bk, bank = next_pb()
                    for hh2 in range(2):
                        hh = hp * 2 + hh2
                        for kc in range(4):
                            add("pe", lambda e, kc=kc, hh=hh, hh2=hh2, bank=bank, view=view: e.matmul(
                                bank[0:64, hh2 * T:(hh2 + 1) * T], lhsT=view[:, kc, hh * 192 + 128:hh * 192 + 192], rhs=cqn[:, kc, :],
                                start=(kc == 0), stop=(kc == 3)), reads=["cqn", sk], writes=[bk])
                    for hh2 in range(2):
                        h = half * 4 + hp * 2 + hh2
                        rope_apply(bank, bk, hh2 * T, qr[0:64, h, :], "qr")
            sk, view, slot_t = next_slab()
            v5 = slot_t[:, :].rearrange("p (k h two d) -> p k h two d", k=2, h=8, two=2)
            for hp in range(4):
                bk, bank = next_pb()
                for hh2 in range(2):
                    h = hp * 2 + hh2
                    for kc in range(2):
                        add("pe", lambda e, kc=kc, h=h, hh2=hh2, bank=bank: e.matmul(
                            bank[:, hh2 * T:(hh2 + 1) * T], lhsT=v5[:, kc, h, 0, :], rhs=ckvn[:, kc, :], start=(kc == 0), stop=(kc == 1)),
                            reads=["ckvn", sk], writes=[bk])
                add("act", lambda e, hp=hp, bank=bank: e.activation(out=Kc[:, 2 * hp:2 * hp + 2, t0:t0 + T], in_=bank[:, :].rearrange("p (c t) -> p c t", c=2), func=AF.Copy),
                    reads=[bk], writes=["Kc"])
            for st in range(2):
                for hg in range(2):
                    bk, bank = next_pb()
                    for kc in range(2):
                        add("pe", lambda e, kc=kc, st=st, hg=hg, bank=bank: e.matmul(
                            bank[:, :].rearrange("p (h d) -> p h d", h=4), lhsT=ckvn[:, kc, st * 128:(st + 1) * 128], rhs=v5[:, kc, hg * 4:(hg + 1) * 4, 1, :],
                            start=(kc == 0), stop=(kc == 1)), reads=["ckvn", sk], writes=[bk])
                    kt = t0 // 128 + st
                    add("act", lambda e, kt=kt, hg=hg, bank=bank: e.activation(out=Vc[:, kt, hg * 512:(hg + 1) * 512], in_=bank[:, :], func=AF.Copy),
                        reads=[bk], writes=["Vc"])

            yield "front"
            steps = [(h, p) for h in range(8) for p in range(i + 1)]

            def emit_S(k):
                h, p = steps[k]
                bk, bank = SB_[k % 2]
                for jj in range(2):
                    j = 2 * p + jj
                    cs = slice(128, T) if (p == i and jj == 1) else slice(0, T)
                    add("pe", lambda e, h=h, j=j, jj=jj, cs=cs, bank=bank: e.matmul(
                        bank[:, jj * T + cs.start:jj * T + cs.stop], lhsT=Kc[:, h, j * 128:(j + 1) * 128], rhs=qn[:, h, cs], start=True, stop=False),
                        reads=["Kc", "qn"], writes=[bk])
                    add("pe", lambda e, h=h, j=j, jj=jj, cs=cs, bank=bank: e.matmul(
                        bank[:, jj * T + cs.start:jj * T + cs.stop], lhsT=KRc[:, j * 128:(j + 1) * 128], rhs=qr[:, h, cs], start=False, stop=True),
                        reads=["KRc", "qr"], writes=[bk])

            def emit_exp(k):
                h, p = steps[k]
                bk, bank = SB_[k % 2]
                ptk = "PT%d" % (k % 2)
                pt = PT[k % 2]
                if p < i:
                    add("act", lambda e: e.activation(out=pt[:, :, :], in_=bank[:, :].rearrange("p (j t) -> p j t", j=2), func=AF.Exp, scale=float(SCALE)),
                        reads=[bk], writes=[ptk])
                else:
                    add("act", lambda e: e.activation(out=pt[:, 0, :], in_=bank[:, 0:T], func=AF.Exp, scale=float(SCALE)), reads=[bk], writes=[ptk])
                    add("act", lambda e: e.activation(out=pt[:, 1, 128:T], in_=bank[:, T + 128:2 * T], func=AF.Exp, scale=float(SCALE)), reads=[bk], writes=[ptk])
                    add("dve", lambda e: e.tensor_tensor(out=pt[:, 0, 0:128], in0=pt[:, 0, 0:128], in1=mask_bf[:], op=ALU.mult), reads=[ptk, "mask_bf"], writes=[ptk])
                    add("dve", lambda e: e.tensor_tensor(out=pt[:, 1, 128:T], in0=pt[:, 1, 128:T], in1=mask_bf[:], op=ALU.mult), reads=[ptk, "mask_bf"], writes=[ptk])

            def emit_PV(k):
                h, p = steps[k]
                ok, obank = OB[h % 2]
                ptk = "PT%d" % (k % 2)
                pt = PT[k % 2]
                for jj in range(2):
                    j = 2 * p + jj
                    cs = slice(128, T) if (p == i and jj == 1) else slice(0, T)
                    firstmm = (p == 0 and jj == 0)
                    lastmm = (p == i and jj == 1)
                    add("pe", lambda e, h=h, j=j, jj=jj, cs=cs, firstmm=firstmm: e.matmul(
                        obank[:, cs], lhsT=Vc[:, j, h * 128:(h + 1) * 128], rhs=pt[:, jj, cs], start=firstmm, stop=False, skip_group_check=True),
                        reads=["Vc", ptk], writes=[ok])
                    add("pe", lambda e, jj=jj, cs=cs, lastmm=lastmm: e.matmul(
                        obank[:, T + cs.start:T + cs.stop], lhsT=ones_bf[:], rhs=pt[:, jj, cs], start=False, stop=lastmm, skip_group_check=True),
                        reads=["ones_bf", ptk], writes=[ok])
                if p == i:
                    add("dve", lambda e: e.reciprocal(out=rc[:], in_=obank[:, T:2 * T]), reads=[ok], writes=["rc"])
                    add("dve", lambda e, h=h: e.tensor_tensor(out=yT[:, 4 + h, :], in0=obank[:, 0:T], in1=rc[:], op=ALU.mult),
                        reads=[ok, "rc"], writes=[("yT", 4 + h)])

            emit_S(0)
            for k in range(len(steps)):
                if k + 1 < len(steps):
                    emit_S(k + 1)
                emit_exp(k)
                emit_PV(k)
                if steps[k][1] == i:
                    yield ("head", steps[k][0])

            gate_chunks = [0, 2, 12, 14, 4, 6, 8, 10]
            for gi, ch0 in enumerate(gate_chunks):
                sk, view, _ = next_slab()
                bk, bank = next_pb()
                proj_fm("hT", sk, view, 2, bank, bk)
                sg = sgt[gi % 2]
                sgk = "sgt%d" % (gi % 2)
                add("act", lambda e, sg=sg, bank=bank: e.activation(out=sg[:, :, :], in_=bank[:, :].rearrange("p (c t) -> p c t", c=2), func=AF.Silu),
                    reads=[bk], writes=[sgk])
                add("dve", lambda e, sg=sg, ch0=ch0: e.tensor_tensor(out=yT[:, ch0:ch0 + 2, :], in0=yT[:, ch0:ch0 + 2, :], in1=sg[:, :, :], op=ALU.mult),
                    reads=[sgk, ("yT", ch0), ("yT", ch0 + 1)], writes=[("yT", ch0), ("yT", ch0 + 1)])

            yield "gates"
            if DEBUG and l == 0:
                add("sp", lambda e: e.dma_start(out=dbg_d[s, i], in_=yT[:]), reads=[("yT", c) for c in range(16)], dma="dbg")
            ensure_ln("post", l)
            for st in range(2):
                if "nospill" in VAR:
                    break
                r0 = t0 + st * 128
                if l == 0:
                    add("pool", lambda e: e.dma_start(out=hs[st][:], in_=h0buf_d[s, r0:r0 + 128, :]), reads=[("h0", s, r0)], writes=["hs%d" % st], dma="ld%d" % st)
                else:
                    add("pool", lambda e: e.dma_start(out=hs[st][:], in_=hbuf_d[s, r0:r0 + 128, :]), reads=[("hbuf", s, r0)], writes=["hs%d" % st], dma="ld%d" % st)
            yall = [("yT", c) for c in range(16)]
            for cb in range(8):
                sk, view, _ = next_slab()
                bk, bank = next_pb()
                for st in range(2):
                    for e_ in range(16):
                        add("pe", lambda e, st=st, e_=e_, bank=bank, view=view: e.matmul(
                            bank[:, st * 256:(st + 1) * 256], lhsT=yT[:, e_, st * 128:(st + 1) * 128], rhs=view[:, e_, :], start=(e_ == 0), stop=False),
                            reads=yall + [sk], writes=[bk])
                    add("pe", lambda e, st=st, cb=cb, bank=bank: e.matmul(
                        bank[:, st * 256:(st + 1) * 256], lhsT=sel33[:, :], rhs=bo[:, cb * 256:(cb + 1) * 256], start=False, stop=True),
                        reads=["sel33", "bo"], writes=[bk])
                for st in range(2):
                    hk = "hs%d" % st
                    add("dve", lambda e, st=st, cb=cb, bank=bank: e.scalar_tensor_tensor(
                        out=hs[st][:, cb * 256:(cb + 1) * 256], in0=hs[st][:, cb * 256:(cb + 1) * 256], scalar=float(ALPHA),
                        in1=bank[:, st * 256:(st + 1) * 256], op0=ALU.mult, op1=ALU.add), reads=[hk, bk], writes=[hk])
            yield "dmm"
            for st in range(2):
                if st == 1:
                    yield "tail0"
                hk = "hs%d" % st
                r0 = t0 + st * 128
                ln_rows(hs[st], hk, 4, 512)
                add("dve", lambda e, st=st: e.scalar_tensor_tensor(out=hs[st][:], in0=hs[st][:], scalar=mv[:, 0:1], in1=lng[:],
                                                                 op0=ALU.subtract, op1=ALU.mult), reads=[hk, "mv", "lng"], writes=[hk])
                add("dve", lambda e, st=st: e.scalar_tensor_tensor(out=hs[st][:], in0=hs[st][:], scalar=rs1[:, 0:1], in1=lnb[:],
                                                                 op0=ALU.mult, op1=ALU.add), reads=[hk, "rs1", "lnb"], writes=[hk])
                if l < L - 1:
                    add("pool", lambda e, st=st, r0=r0: e.dma_start(out=hbuf_d[s, r0:r0 + 128, :], in_=hs[st][:]),
                        reads=[hk], writes=[("hbuf", s, r0)], dma="st%d" % st)
                else:
                    add("pool", lambda e, st=st, r0=r0: e.dma_start(out=out_d[s, r0:r0 + 128, :], in_=hs[st][:]),
                        reads=[hk], writes=[("out", s, r0)], dma="st%d" % st)

        stage0a(*order[0], 0)
        cur_layer[0] = 0
        layer_setup(0)
        rope_chunk(order[0][0], order[0][2])
        stage0a(*order[0], 1)
        stage0b(*order[0])
        conv_layers([0])
        for l_ in range(1, L):
            pending_conv.extend((l_, si) for si in range(NSLAB))
        prev_tail = None
        for n_, (s, l, i) in enumerate(order):
            nxt = order[n_ + 1] if n_ + 1 < len(order) else None
            g_ = tile_gen(s, l, i)
            next(g_)
            next(g_)
            next(g_)
            while True:
                r = next(g_)
                if r == "gates":
                    break
                h = r[1]
                if h == 0 and prev_tail is not None:
                    next(prev_tail)
                elif h == 1:
                    if prev_tail is not None:
                        for _ in prev_tail:
                            pass
                    if nxt is not None:
                        stage0a(*nxt, 0)
                elif h == 3 and nxt is not None:
                    stage0a(*nxt, 1, (0,))
                elif h == 5 and nxt is not None:
                    stage0a(*nxt, 1, (1,))
                elif h == 6 and nxt is not None and nxt[1] == 0:
                    rope_chunk(nxt[0], nxt[2])
            if nxt is not None:
                stage0b(*nxt)
            next(g_)
            prev_tail = g_
        for _ in prev_tail:
            pass

        fin = add("sp", None)
        for op in S_.ops:
            if op["dma"] in ("st0", "st1"):
                S_.ops[fin]["deps"].add(op["idx"])
        S_.emit(nc, ctx)
    return nc


def host_consts(L, q_norm_g, kv_norm_g, pool_scale, sgu_norm_g, sgu_norm_b):
    pp = np.zeros((128, L, 20), np.float32)
    pp[:, :, 0:4] = q_norm_g.reshape(L, 4, 128).transpose(2, 0, 1)
    pp[:, :, 4:6] = kv_norm_g.reshape(L, 2, 128).transpose(2, 0, 1)
    pp[:, :, 6:10] = pool_scale.reshape(L, 4, 128).transpose(2, 0, 1)
    pp[:, :, 10:14] = sgu_norm_g.reshape(L, 4, 128).transpose(2, 0, 1)
    pp[:, :, 14:18] = sgu_norm_b.reshape(L, 4, 128).transpose(2, 0, 1)
    ident = np.eye(128, dtype=np.float32)
    mask = (np.arange(128)[:, None] <= np.arange(128)[None, :]).astype(np.float32)
    invc = np.zeros((4, 16), np.float32)
    for g in range(4):
        w = 2 << g
        invc[g] = 1.0 / np.minimum(np.arange(1, 17), w).astype(np.float32)
    half = 32
    inv_freq = (np.float32(10000.0) ** (-np.arange(half, dtype=np.float32) / np.float32(half))).astype(np.float32)
    crope = np.zeros((64, 4), np.float32)
    crope[:32, 0] = inv_freq
    crope[32:, 0] = inv_freq
    crope[:32, 1] = -1.0
    crope[32:, 1] = 1.0
    return pp, ident, mask, invc.reshape(64), crope


_CACHE = {}


def run(inputs, NSEQ, S, L, ncores):
    key = (NSEQ, S, L)
    if key not in _CACHE:
        _CACHE[key] = build_program(NSEQ, S, L)
    nc = _CACHE[key]
    f = lambda a: np.ascontiguousarray(np.asarray(a, dtype=np.float32))
    pp, ident, mask, invc, crope = host_consts(L, f(inputs["q_norm_g"]), f(inputs["kv_norm_g"]), f(inputs["pool_scale"]),
                                               f(inputs["sgu_norm_g"]), f(inputs["sgu_norm_b"]))
    x = f(inputs["x"])
    pos = np.ascontiguousarray(np.asarray(inputs["positions"], dtype=np.int32))
    shared = dict(
        ln_in_g=f(inputs["ln_in_g"]), ln_in_b=f(inputs["ln_in_b"]), w_in=f(inputs["w_in"]), pool_w=f(inputs["pool_w"]),
        w_uq=f(inputs["w_uq"]), w_ukv=f(inputs["w_ukv"]),
        sgu_wT=np.ascontiguousarray(f(inputs["sgu_w"]).transpose(0, 1, 3, 2)),
        sgu_b=f(inputs["sgu_b"]).reshape(L, 512), w_out=f(inputs["w_out"]), b_out=f(inputs["b_out"]),
        ln_post_g=f(inputs["ln_post_g"]), ln_post_b=f(inputs["ln_post_b"]),
        pparams=pp, c_ident=ident, c_mask=mask, c_invc=invc, c_rope=crope)
    in_maps = []
    for c in range(ncores):
        m = dict(shared)
        m["x"] = np.ascontiguousarray(x[c * NSEQ:(c + 1) * NSEQ])
        m["positions"] = np.ascontiguousarray(pos[c * NSEQ:(c + 1) * NSEQ])
        in_maps.append(m)
    res = run_bass_kernel_spmd(nc, in_maps, core_ids=list(range(ncores)))
    if DEBUG:
        global _DBG
        _DBG = [np.asarray(r["dbg"]) for r in res.results]
    return np.concatenate([np.asarray(r["out"], dtype=np.float32) for r in res.results], axis=0)


def kernel(**inputs):
    x = np.asarray(inputs["x"])
    B, S, _ = x.shape
    L = np.asarray(inputs["w_in"]).shape[0]
    ncores = 8
    return run(inputs, B // ncores, S, L, ncores)
```
